# Optimizing a Trainium2 kernel written in Bass

```python
import math
import jax, jax.numpy as jnp
from jax import lax
import numpy as np

D_MODEL = 2048
BATCH = 2
SEQ = 4096
DEPTH = 1

GRID_W = 64
CTX_LEN = 256
N_ADA = 9
D_FF = 5632
SSM_WIDTH = 1024
SSM_GROUP = 16
SSM_GROUPS = SSM_WIDTH // SSM_GROUP
SSM_STATE = 64
RET_HEADS = 8
RET_DK = 128
RET_DV = 256
RET_QK_WIDTH = RET_HEADS * RET_DK
RET_V_WIDTH = RET_HEADS * RET_DV
RET_CHUNK = 128
ROPE_BASE = 10000.0
NORM_EPS = 1e-6
MIX_SPLITS = (SSM_WIDTH, RET_QK_WIDTH, RET_QK_WIDTH, RET_V_WIDTH, RET_V_WIDTH, D_MODEL, D_MODEL)
MIX_IN_WIDTH = sum(MIX_SPLITS)

kernel_name = "hybrid_s5_retention_macaron_dit_layer"


def rmsnorm(h, g):
    hf = h.astype(jnp.float32)
    hf = hf * lax.rsqrt(jnp.mean(hf * hf, axis=-1, keepdims=True) + NORM_EPS)
    return (hf * g.astype(jnp.float32)).astype(h.dtype)


def head_rmsnorm(o):
    of = o.astype(jnp.float32)
    return of * lax.rsqrt(jnp.mean(of * of, axis=-1, keepdims=True) + NORM_EPS)


def ada_pre(h, ada, i, g):
    return rmsnorm(h, g) * (1.0 + ada[:, :, 3 * i + 1]) + ada[:, :, 3 * i]


def ada_post(h, out, ada, i, g, res_w):
    return (h + res_w * ada[:, :, 3 * i + 2] * rmsnorm(out, g)).astype(h.dtype)


def swiglu(h, w_in, w_out):
    gt, up = jnp.split(h @ w_in, 2, axis=-1)
    return (jax.nn.silu(gt) * up) @ w_out


def _flip(t, axis, rev):
    return jnp.flip(t, axis=axis) if rev else t


def rope_1d(t, pos):
    half = t.shape[-1] // 2
    inv = ROPE_BASE ** (-jnp.arange(half, dtype=jnp.float32) / half)
    ang = pos.astype(jnp.float32)[:, None] * inv
    cos = jnp.cos(ang).astype(t.dtype)
    sin = jnp.sin(ang).astype(t.dtype)
    t1, t2 = t[..., :half], t[..., half:]
    return jnp.concatenate([t1 * cos - t2 * sin, t1 * sin + t2 * cos], axis=-1)


def rope_2d(t, row, col):
    half = t.shape[-1] // 2
    return jnp.concatenate([rope_1d(t[..., :half], row), rope_1d(t[..., half:], col)], axis=-1)


def split_heads(t, dh):
    b, l, _ = t.shape
    return t.reshape(b, l, -1, dh).transpose(0, 2, 1, 3)


def merge_heads(o):
    b, h, l, dh = o.shape
    return o.transpose(0, 2, 1, 3).reshape(b, l, h * dh)


def complex_affine_combine(e1, e2):
    a1r, a1i, b1r, b1i = e1
    a2r, a2i, b2r, b2i = e2
    ar = a2r * a1r - a2i * a1i
    ai = a2r * a1i + a2i * a1r
    br = a2r * b1r - a2i * b1i + b2r
    bi = a2r * b1i + a2i * b1r + b2i
    return ar, ai, br, bi


def s5_scan(u, lam_re, lam_im, log_step, b_re, b_im, c_re, c_im, h0_re, h0_im, with_output):
    f32 = jnp.float32
    lam_re, lam_im = lam_re.astype(f32), lam_im.astype(f32)
    b_re, b_im = b_re.astype(f32), b_im.astype(f32)
    step = jnp.exp(log_step.astype(f32))[:, None]
    mag = jnp.exp(lam_re * step)
    a_re, a_im = mag * jnp.cos(lam_im * step), mag * jnp.sin(lam_im * step)
    den = lam_re * lam_re + lam_im * lam_im
    num_re, num_im = a_re - 1.0, a_im
    k_re = (num_re * lam_re + num_im * lam_im) / den
    k_im = (num_im * lam_re - num_re * lam_im) / den
    bb_re = k_re[..., None] * b_re - k_im[..., None] * b_im
    bb_im = k_re[..., None] * b_im + k_im[..., None] * b_re
    x_re = jnp.einsum('blgc,gpc->lbgp', u, bb_re)
    x_im = jnp.einsum('blgc,gpc->lbgp', u, bb_im)
    x_re = x_re.at[0].add(a_re * h0_re - a_im * h0_im)
    x_im = x_im.at[0].add(a_re * h0_im + a_im * h0_re)
    seq_len = u.shape[1]
    ar = jnp.broadcast_to(a_re, (seq_len, 1) + a_re.shape)
    ai = jnp.broadcast_to(a_im, (seq_len, 1) + a_im.shape)
    _, _, h_re, h_im = lax.associative_scan(complex_affine_combine, (ar, ai, x_re, x_im), axis=0)
    final = (h_re[-1], h_im[-1])
    if not with_output:
        return None, final
    y = (jnp.einsum('lbgp,gcp->blgc', h_re, c_re.astype(f32))
         - jnp.einsum('lbgp,gcp->blgc', h_im, c_im.astype(f32)))
    return y, final


def retention_chunked(q, k, v, log_gamma, s0, strict, with_output):
    b, h, seq_len, dk = k.shape
    dv = v.shape[-1]
    n = seq_len // RET_CHUNK
    kc = k.reshape(b, h, n, RET_CHUNK, dk)
    vc = v.reshape(b, h, n, RET_CHUNK, dv)
    pos = jnp.arange(RET_CHUNK, dtype=jnp.float32)
    lg = log_gamma[:, None]
    w_end = jnp.exp(lg * (RET_CHUNK - 1.0 - pos))
    kv = jnp.einsum('bhncd,bhnce->nbhde', kc * w_end[None, :, None, :, None], vc)
    g_chunk = jnp.exp(log_gamma * RET_CHUNK)[None, :, None, None]

    def step(s, inc):
        return g_chunk * s + inc, s

    s_final, s_in = lax.scan(step, s0, kv)
    if not with_output:
        return None, s_final
    qc = q.reshape(b, h, n, RET_CHUNK, dk)
    diff = pos[:, None] - pos[None, :]
    mask = diff > 0 if strict else diff >= 0
    decay = jnp.where(mask, jnp.exp(lg[:, :, None] * jnp.where(mask, diff, 0.0)), 0.0)
    scores = jnp.einsum('bhnid,bhnjd->bhnij', qc, kc) * decay[None, :, None]
    o = jnp.einsum('bhnij,bhnje->bhnie', scores, vc)
    w_in = jnp.exp(lg * (pos + 1.0))
    o = o + jnp.einsum('bhnid,nbhde->bhnie', qc * w_in[None, :, None, :, None], s_in)
    return o.reshape(b, h, seq_len, dv), s_final


def merge_branches(y_ssm, o_ret, g, gs, gr, ssm_glu_w, ret_w_proj, mix_w_out):
    a = jax.nn.gelu(y_ssm)
    ga, gb = jnp.split(a @ ssm_glu_w, 2, axis=-1)
    ssm_branch = ga * jax.nn.sigmoid(gb)
    ret_branch = (jax.nn.silu(g) * o_ret) @ ret_w_proj
    merged = jax.nn.sigmoid(gs) * ssm_branch + jax.nn.sigmoid(gr) * ret_branch
    return merged @ mix_w_out


def token_mixer(u_x, u_c, w_in, lam_re, lam_im, log_step, b_re, b_im, c_re, c_im, d_skip,
                glu_w, decay_logit, ret_w_proj, w_out, with_ctx_out):
    f32 = jnp.float32
    b, seq_len, _ = u_x.shape
    ctx_len = u_c.shape[1]
    dt = u_x.dtype
    rows = seq_len // GRID_W
    row = jnp.repeat(jnp.arange(rows, dtype=jnp.int32), GRID_W)
    col = jnp.tile(jnp.arange(GRID_W, dtype=jnp.int32), rows)
    cuts = [int(v) for v in np.cumsum(MIX_SPLITS)[:-1]]
    s_x, q_x, k_x, v_x, g_x, gs_x, gr_x = jnp.split(u_x @ w_in, cuts, axis=-1)
    s_c, q_c, k_c, v_c, g_c, gs_c, gr_c = jnp.split(u_c @ w_in, cuts, axis=-1)

    us_x = s_x.astype(f32).reshape(b, seq_len, SSM_GROUPS, SSM_GROUP)
    us_c = s_c.astype(f32).reshape(b, ctx_len, SSM_GROUPS, SSM_GROUP)
    dg = d_skip.astype(f32).reshape(SSM_GROUPS, SSM_GROUP)
    y_x = dg * us_x
    y_c = dg * us_c
    h_zero = jnp.zeros((b, SSM_GROUPS, SSM_STATE), f32)
    for d in range(2):
        rev = d == 1
        prm = (lam_re[d], lam_im[d], log_step[d], b_re[d], b_im[d], c_re[d], c_im[d])
        yc_d, (hc_re, hc_im) = s5_scan(_flip(us_c, 1, rev), *prm, h_zero, h_zero, with_ctx_out)
        yx_d, _ = s5_scan(_flip(us_x, 1, rev), *prm, hc_re, hc_im, True)
        y_x = y_x + _flip(yx_d, 1, rev)
        if with_ctx_out:
            y_c = y_c + _flip(yc_d, 1, rev)

    q_scale = RET_DK ** -0.5
    qx = rope_2d(split_heads(q_x, RET_DK), row, col) * q_scale
    kx = rope_2d(split_heads(k_x, RET_DK), row, col)
    vx = split_heads(v_x, RET_DV)
    qc = split_heads(q_c, RET_DK) * q_scale
    kc = split_heads(k_c, RET_DK)
    vc = split_heads(v_c, RET_DV)
    log_gamma = jax.nn.log_sigmoid(decay_logit.astype(f32))
    s_zero = jnp.zeros((b, RET_HEADS, RET_DK, RET_DV), f32)
    ox_dirs = []
    oc_dirs = []
    for d in range(2):
        rev = d == 1
        oc_d, s_ctx = retention_chunked(_flip(qc, 2, rev), _flip(kc, 2, rev), _flip(vc, 2, rev),
                                        log_gamma[d], s_zero, rev, with_ctx_out)
        ox_d, _ = retention_chunked(_flip(qx, 2, rev), _flip(kx, 2, rev), _flip(vx, 2, rev),
                                    log_gamma[d], s_ctx, rev, True)
        ox_dirs.append(_flip(ox_d, 2, rev))
        if with_ctx_out:
            oc_dirs.append(_flip(oc_d, 2, rev))
    o_x = merge_heads(head_rmsnorm(ox_dirs[0] + ox_dirs[1])).astype(dt)

    out_x = merge_branches(y_x.reshape(b, seq_len, SSM_WIDTH).astype(dt), o_x, g_x, gs_x, gr_x,
                           glu_w, ret_w_proj, w_out)
    if not with_ctx_out:
        return out_x, None
    o_c = merge_heads(head_rmsnorm(oc_dirs[0] + oc_dirs[1])).astype(dt)
    out_c = merge_branches(y_c.reshape(b, ctx_len, SSM_WIDTH).astype(dt), o_c, g_c, gs_c, gr_c,
                           glu_w, ret_w_proj, w_out)
    return out_x, out_c


def setup_inputs(seed: int = 0) -> dict:
    key = jax.random.key(seed)
    ks = jax.random.split(key, 24)
    f32 = jnp.float32
    G, P, H = SSM_GROUPS, SSM_STATE, RET_HEADS

    def normal(k, shape, scale):
        return jax.random.normal(k, shape, f32) * scale

    n_idx = jnp.arange(P, dtype=f32)
    heads = jnp.arange(H, dtype=f32)
    decay_base = jnp.log(2.0 ** (5.0 + heads) - 1.0)
    return {
        "x": normal(ks[0], (BATCH, SEQ, D_MODEL), 1.0),
        "c": normal(ks[1], (BATCH, D_MODEL), 1.0),
        "ctx": normal(ks[2], (BATCH, CTX_LEN, D_MODEL), 1.0),
        "c_ctx": normal(ks[3], (D_MODEL,), 1.0),
        "ada_w": normal(ks[4], (DEPTH, D_MODEL, N_ADA * D_MODEL), 0.5 * D_MODEL ** -0.5),
        "ada_b": normal(ks[5], (DEPTH, N_ADA * D_MODEL), 0.02),
        "norm_g": 1.0 + normal(ks[6], (DEPTH, 6, D_MODEL), 0.02),
        "ffn_w_in": normal(ks[7], (DEPTH, 2, D_MODEL, 2 * D_FF), D_MODEL ** -0.5),
        "ffn_w_out": normal(ks[8], (DEPTH, 2, D_FF, D_MODEL), D_FF ** -0.5),
        "mix_w_in": normal(ks[9], (DEPTH, D_MODEL, MIX_IN_WIDTH), D_MODEL ** -0.5),
        "ssm_lam_re": -0.5 + normal(ks[10], (DEPTH, 2, G, P), 0.01),
        "ssm_lam_im": math.pi * n_idx + normal(ks[11], (DEPTH, 2, G, P), 0.01),
        "ssm_log_step": jax.random.uniform(ks[12], (DEPTH, 2, G), f32, math.log(1e-3), math.log(1e-1)),
        "ssm_b_re": normal(ks[13], (DEPTH, 2, G, P, SSM_GROUP), (2 * SSM_GROUP) ** -0.5),
        "ssm_b_im": normal(ks[14], (DEPTH, 2, G, P, SSM_GROUP), (2 * SSM_GROUP) ** -0.5),
        "ssm_c_re": normal(ks[15], (DEPTH, 2, G, SSM_GROUP, P), P ** -0.5),
        "ssm_c_im": normal(ks[16], (DEPTH, 2, G, SSM_GROUP, P), P ** -0.5),
        "ssm_d": normal(ks[17], (DEPTH, SSM_WIDTH), 1.0),
        "ssm_glu_w": normal(ks[18], (DEPTH, SSM_WIDTH, 2 * D_MODEL), SSM_WIDTH ** -0.5),
        "ret_decay_logit": decay_base + normal(ks[19], (DEPTH, 2, H), 0.05),
        "ret_w_proj": normal(ks[20], (DEPTH, RET_V_WIDTH, D_MODEL), RET_V_WIDTH ** -0.5),
        "mix_w_out": normal(ks[21], (DEPTH, D_MODEL, D_MODEL), D_MODEL ** -0.5),
    }


def reference(x, c, ctx, c_ctx, ada_w, ada_b, norm_g, ffn_w_in, ffn_w_out, mix_w_in,
              ssm_lam_re, ssm_lam_im, ssm_log_step, ssm_b_re, ssm_b_im, ssm_c_re, ssm_c_im,
              ssm_d, ssm_glu_w, ret_decay_logit, ret_w_proj, mix_w_out):
    b = x.shape[0]
    for l in range(DEPTH):
        last = l == DEPTH - 1
        g = norm_g[l]
        ada_x = (jax.nn.silu(c) @ ada_w[l] + ada_b[l]).reshape(b, 1, N_ADA, D_MODEL)
        ada_c = (jax.nn.silu(c_ctx) @ ada_w[l] + ada_b[l]).reshape(1, 1, N_ADA, D_MODEL)

        x = ada_post(x, swiglu(ada_pre(x, ada_x, 0, g[0]), ffn_w_in[l, 0], ffn_w_out[l, 0]), ada_x, 0, g[1], 0.5)
        ctx = ada_post(ctx, swiglu(ada_pre(ctx, ada_c, 0, g[0]), ffn_w_in[l, 0], ffn_w_out[l, 0]), ada_c, 0, g[1], 0.5)

        u_x = ada_pre(x, ada_x, 1, g[2])
        u_c = ada_pre(ctx, ada_c, 1, g[2])
        mix_x, mix_c = token_mixer(u_x, u_c, mix_w_in[l], ssm_lam_re[l], ssm_lam_im[l], ssm_log_step[l],
                                   ssm_b_re[l], ssm_b_im[l], ssm_c_re[l], ssm_c_im[l], ssm_d[l],
                                   ssm_glu_w[l], ret_decay_logit[l], ret_w_proj[l], mix_w_out[l],
                                   not last)
        x = ada_post(x, mix_x, ada_x, 1, g[3], 1.0)

        x = ada_post(x, swiglu(ada_pre(x, ada_x, 2, g[4]), ffn_w_in[l, 1], ffn_w_out[l, 1]), ada_x, 2, g[5], 0.5)
        if not last:
            ctx = ada_post(ctx, mix_c, ada_c, 1, g[3], 1.0)
            ctx = ada_post(ctx, swiglu(ada_pre(ctx, ada_c, 2, g[4]), ffn_w_in[l, 1], ffn_w_out[l, 1]), ada_c, 2, g[5], 0.5)
    return x
```

```python
import os
import numpy as np
import concourse.bass as bass
import concourse.mybir as mybir
from concourse.bass_utils import run_bass_kernel_spmd
from contextlib import ExitStack

F32 = mybir.dt.float32
BF16 = mybir.dt.bfloat16
AF = mybir.ActivationFunctionType
ALU = mybir.AluOpType

D = 2048
KC = 16
T = 1088
TL = 1024
TCX = 64
DFF = 5632
HC = 44
NH = 4
HG = HC // NH
NADA = 9
EPS = 1e-6
TB = [(0, 512), (512, 512), (1024, 64)]
TBL = [(0, 512), (512, 512)]
NB = 136
NBL = 128
STAGE = int(os.environ.get("MK_STAGE", "99"))
MODE = os.environ.get("MK_MODE", "full")
SKIP = os.environ.get("MK_SKIP", "")


class Trk:
    __slots__ = ("w", "r", "sem")

    def __init__(self):
        self.w = {}
        self.r = {}
        self.sem = None


class Ring:
    def __init__(self, items):
        self.items = items
        self.i = 0

    def next(self):
        it = self.items[self.i % len(self.items)]
        self.i += 1
        return it


class KB:
    def __init__(self, nc, es):
        self.nc = nc
        self.es = es
        self.eng = {"pe": nc.tensor, "dve": nc.vector, "act": nc.scalar, "pool": nc.gpsimd, "sp": nc.sync}
        self.sem = {}
        self.cnt = {}
        for n in ["pe", "dve", "act", "pool", "cc"]:
            self.newsem(n)
        self.seen = {e: {} for e in self.eng}
        self.nds = 0
        self.nm = 0

    def newsem(self, n):
        self.sem[n] = self.es.enter_context(self.nc.semaphore("s_" + n))
        self.cnt[n] = 0

    def sb(self, shape, dt, es=None, name=None):
        self.nm += 1
        return (es or self.es).enter_context(self.nc.sbuf_tensor(name or ("t%d" % self.nm), shape, dt))

    def ps(self, shape, dt=F32):
        self.nm += 1
        return self.es.enter_context(self.nc.psum_tensor("p%d" % self.nm, shape, dt))

    def _wait(self, e, deps, skip=None):
        for s, v in deps.items():
            if s == skip or v <= 0:
                continue
            if self.seen[e].get(s, 0) >= v:
                continue
            self.eng[e].wait_ge(self.sem[s], v)
            self.seen[e][s] = v

    def op(self, e, fn, R=(), W=(), dsem=None, cc=False):
        deps = {}

        def add(d):
            for s, v in d.items():
                if deps.get(s, 0) < v:
                    deps[s] = v
        for t in R:
            add(t.w)
        for t in W:
            add(t.w)
            add(t.r)
        self._wait(e, deps, skip=("pe" if e == "pe" else None))
        ins = fn(self.eng[e])
        if cc:
            s, inc = "cc", 1
        elif dsem is not None:
            if dsem.sem is None:
                self.nds += 1
                dsem.sem = "d%d" % self.nds
                self.newsem(dsem.sem)
            s, inc = dsem.sem, 16
        else:
            s, inc = e, 1
        self.cnt[s] += inc
        ins.then_inc(self.sem[s], inc)
        v = self.cnt[s]
        for t in R:
            if t.r.get(s, 0) < v:
                t.r[s] = v
        for t in W:
            t.w = {s: v} if not (dsem is not None and False) else t.w
            t.r = {}
        return (s, v)

    def opw(self, e, fn, R=(), W=(), dsem=None, keep=(), cc=False):
        old = {id(t): (dict(t.w), dict(t.r)) for t in keep}
        s, v = self.op(e, fn, R, W, dsem, cc)
        for t in keep:
            d, r = old[id(t)]
            d[s] = max(d.get(s, 0), v)
            t.w = d
            for s2, v2 in t.r.items():
                if r.get(s2, 0) < v2:
                    r[s2] = v2
            t.r = r

    def barrier(self):
        for e in self.eng:
            for s, v in self.cnt.items():
                if v > 0 and self.seen[e].get(s, 0) < v:
                    self.eng[e].wait_ge(self.sem[s], v)
                    self.seen[e][s] = v


def mkring(k, n, shape, dt, es=None):
    return Ring([(k.sb(shape, dt, es), Trk()) for _ in range(n)])


class Ctx:
    pass


def build():
    nc = bass.Bass("TRN2", target_bir_lowering=False)
    g = Ctx()

    def din(name, shape, dt=F32):
        return nc.dram_tensor(name, list(shape), dt, kind="ExternalInput").ap()

    def dbig(name, shape):
        return din(name, shape) if MODE == "full" else None
    xT = dbig("xT", [128, KC, T])
    cT = dbig("cT", [128, KC, 2])
    ada_w = dbig("ada_w", [NADA * KC, 128, KC * 128])
    ada_b = dbig("ada_b", [128, NADA * KC])
    norm_g = dbig("norm_g", [128, 6, KC])
    ffn_in = dbig("ffn_in", [2, 2 * HC, 128, KC * 128])
    ffn_out = dbig("ffn_out", [2, NH * KC, 128, HG * 128])
    ones_d = din("ones", [128, 128])
    dbg_n = int(os.environ.get("MK_DBG", "0"))
    outT = nc.dram_tensor("outT", [128, KC, TL], F32, kind="ExternalOutput").ap()
    dbg = None
    if dbg_n:
        dbg = nc.dram_tensor("dbg", [128, KC, T], F32, kind="ExternalOutput").ap()
    x1s = nc.dram_tensor("x1s", [128, KC, T], F32).ap()

    def dmix(name, shape):
        return din(name, shape) if MODE in ("full", "mix") else None
    mix_in = dmix("mix_in", [88, 128, KC * 128])
    glu_w = dmix("glu_w", [32, 128, 8 * 128])
    retp_w = dmix("retp_w", [16, 128, KC * 128])
    mixo_w = dmix("mixo_w", [16, 128, KC * 128])
    s5lam_d = din("s5lam", [128, 3, 2, 32])
    s5b_d = din("s5b", [128, 2, 2, 32, 16])
    s5c_d = din("s5c", [128, 2, 2, 32, 16])
    s5d_d = din("s5d", [128, 64])
    cmat_d = din("cmat", [128, 8, 128])
    cvec_d = din("cvec", [128, 8])
    selj_d = din("selj", [128, 4])
    selA_d = din("selA", [128, 64, 128])
    selB_d = din("selB", [128, 64, 128])
    decay_d = din("decay", [128, 16])
    ropec_d = dmix("ropec", [128, T])
    ropes_d = dmix("ropes", [128, T])
    pmat_d = dmix("pmat", [128, 128])
    dbgm = nc.dram_tensor("dbgm", [128, KC, TL], F32, kind="ExternalOutput").ap() if MODE == "mix" else None

    def scr(name, shape, dt, tin=(), tout=()):
        if MODE in tin:
            return nc.dram_tensor(name, list(shape), dt, kind="ExternalInput").ap()
        if MODE in tout:
            return nc.dram_tensor(name, list(shape), dt, kind="ExternalOutput").ap()
        return nc.dram_tensor(name, list(shape), dt).ap()
    x2s = scr("x2s", [128, KC, TL], F32)
    mrs = scr("mrs", [128, KC, TL], F32)
    u2s = scr("u2s", [128, KC, T], BF16, tin=("mix",))
    u3s = scr("u3s", [128, KC, TL], BF16)
    sTd = scr("sTd", [128, 8, T], BF16, tin=("s5",))
    qTd = scr("qTd", [128, 8, TL], BF16, tin=("ret",))
    kTd = scr("kTd", [128, 8, T], BF16, tin=("ret",))
    ktokd = scr("ktokd", [9, 128, 1024], BF16, tin=("ret",))
    vtokd = scr("vtokd", [9, 128, 2048], BF16, tin=("ret",))
    aTd = scr("aTd", [128, 8, TL], BF16, tout=("s5",))
    oTd = scr("oTd", [128, KC, TL], BF16, tout=("ret",))
    Gd = scr("Gd", [128, 64 * 2 * 2 * 64], BF16)
    Wd = scr("Wd", [128, 2 * 2 * 32 * 128], BF16)
    Td = scr("Td", [128, 64 * 128], BF16)
    pub5_t = nc.dram_tensor("pub5", [128, 256], F32)
    gat5_t = nc.dram_tensor("gat5", [512, 256], F32)
    pubRt_ = [nc.dram_tensor("pubR%d" % h, [128, 1024], F32) for h in range(8)]
    gatRt_ = [nc.dram_tensor("gatR%d" % h, [512, 1024], F32) for h in range(8)]
    pub5, gat5 = pub5_t.ap(), gat5_t.ap()

    with ExitStack() as es:
        k = KB(nc, es)
        es.enter_context(nc.Block())
        k.psum = Ring([(k.ps([128, 512], F32), Trk()) for _ in range(6)])
        psbf = Ring([(k.ps([128, 1024], BF16)[:, 0:128], Trk()) for _ in range(2)])
        (t_x2s, t_mrs, t_u2s, t_u3s, t_sTd, t_qTd, t_kTd, t_ktok, t_vtok, t_aTd, t_oTd, t_Gd, t_Wd, t_Td,
         t_pub5, t_gat5, t_pubR, t_gatR) = [Trk() for _ in range(18)]
        t_in = Trk()
        t_x1s = Trk()
        t_out = Trk()
        t_dbg = Trk()

        ones_f = k.sb([128, 128], F32)
        ones_b = k.sb([128, 128], BF16)
        t_ones = Trk()
        k.op("sp", lambda e: e.dma_start(out=ones_f[:, :], in_=ones_d[:, :]), W=[t_ones], dsem=t_ones)
        k.op("dve", lambda e: e.tensor_copy(out=ones_b[:, :], in_=ones_f[:, :]), R=[t_ones], W=[t_ones])
        ada = k.sb([128, NADA * KC, 2], F32)
        t_ada = Trk()
        coefA = k.sb([128, 3, KC, 2], F32)
        coefS = k.sb([128, 3, KC, 2], F32)
        coefG = k.sb([128, 3, KC, 2], F32)
        t_coef = Trk()
        wring = mkring(k, 4, [128, KC * 128], BF16)

        def lin(wd, ccs, kcn, rhs_fn, blocks, epi):
            for cc in ccs:
                wt, wtr = wring.next()
                k.op("pool", lambda e: e.dma_start(out=wt[:, :kcn * 128], in_=wd[cc]), R=[t_in], W=[wtr], dsem=wtr)
                for bi, (t0, n) in enumerate(blocks):
                    ps, ptr = k.psum.next()
                    for kc in range(kcn):
                        rhs, rtr = rhs_fn(kc, t0, n)
                        k.op("pe", lambda e: e.matmul(ps[:, :n], wt[:, kc * 128:(kc + 1) * 128], rhs,
                                                      start=(kc == 0), stop=(kc == kcn - 1)), R=[wtr, rtr], W=[ptr])
                    epi(cc, bi, t0, n, ps, ptr)

        if MODE == "full":
            with ExitStack() as ph:
                cs = k.sb([128, KC, 2], F32, ph)
                cb = k.sb([128, KC, 2], BF16, ph)
                ab = k.sb([128, NADA * KC], F32, ph)
                ng = k.sb([128, 6, KC], F32, ph)
                t_c, t_ab, t_ng = Trk(), Trk(), Trk()
                k.op("sp", lambda e: e.dma_start(out=cs[:, :, :], in_=cT[:, :, :]), W=[t_c], dsem=t_c)
                k.op("sp", lambda e: e.dma_start(out=ab[:, :], in_=ada_b[:, :]), W=[t_ab], dsem=t_ab)
                k.op("sp", lambda e: e.dma_start(out=ng[:, :, :], in_=norm_g[:, :, :]), W=[t_ng], dsem=t_ng)
                k.op("act", lambda e: e.activation(out=cb[:, :, :], in_=cs[:, :, :], func=AF.Silu), R=[t_c], W=[t_c])

                def epi_ada(cc, bi, t0, n, ps, ptr):
                    k.opw("dve", lambda e: e.tensor_scalar(out=ada[:, cc, :], in0=ps[:, 0:2], scalar1=ab[:, cc:cc + 1], scalar2=None,
                                                           op0=ALU.add), R=[ptr, t_ab], W=[t_ada], keep=[t_ada])
                lin(ada_w, range(NADA * KC), KC, lambda kc, t0, n: (cb[:, kc, :], t_c), [(0, 2)], epi_ada)
                for i in range(3):
                    sh = ada[:, (3 * i) * KC:(3 * i + 1) * KC, :]
                    sc = ada[:, (3 * i + 1) * KC:(3 * i + 2) * KC, :]
                    gt = ada[:, (3 * i + 2) * KC:(3 * i + 3) * KC, :]
                    gpre = ng[:, 2 * i, :].unsqueeze(2).broadcast_to([128, KC, 2])
                    gpost = ng[:, 2 * i + 1, :].unsqueeze(2).broadcast_to([128, KC, 2])
                    rw = 1.0 if i == 1 else 0.5
                    k.opw("dve", lambda e: e.scalar_tensor_tensor(out=coefA[:, i, :, :], in0=sc, scalar=1.0, in1=gpre, op0=ALU.add, op1=ALU.mult),
                          R=[t_ada, t_ng], W=[t_coef], keep=[t_coef])
                    k.opw("dve", lambda e: e.tensor_copy(out=coefS[:, i, :, :], in_=sh), R=[t_ada], W=[t_coef], keep=[t_coef])
                    k.opw("dve", lambda e: e.scalar_tensor_tensor(out=coefG[:, i, :, :], in0=gt, scalar=rw, in1=gpost, op0=ALU.mult, op1=ALU.mult),
                          R=[t_ada, t_ng], W=[t_coef], keep=[t_coef])
                k.barrier()

        def rstd_of(src_fn, src_tr, blocks, rstd, t_rstd, sq_ring, dcount):
            for (t0, n) in blocks:
                ps, ptr = k.psum.next()
                for kc in range(KC):
                    sq, sqt = sq_ring.next()
                    k.op("act", lambda e: e.activation(out=sq[:, :n], in_=src_fn(kc, t0, n), func=AF.Square), R=[src_tr], W=[sqt])
                    k.op("pe", lambda e: e.matmul(ps[:, :n], ones_b[:, :], sq[:, :n], start=(kc == 0), stop=(kc == KC - 1)),
                         R=[sqt, t_ones], W=[ptr])
                k.opw("act", lambda e: e.activation(out=rstd[:, t0:t0 + n], in_=ps[:, :n], func=AF.Sqrt, scale=1.0 / dcount, bias=epsb[:, 0:1]),
                      R=[ptr, t_eps], W=[t_rstd], keep=[t_rstd])
                k.opw("dve", lambda e: e.reciprocal(out=rstd[:, t0:t0 + n], in_=rstd[:, t0:t0 + n]), R=[t_rstd], W=[t_rstd], keep=[t_rstd])

        epsb = k.sb([128, 1], F32)
        t_eps = Trk()
        k.op("dve", lambda e: e.memset(epsb[:, :], EPS), W=[t_eps])

        def ada_pre(i, xs, t_xs, blocks, u, t_u, rstd, t_rstd, sq_ring, tmp_ring):
            rstd_of(lambda kc, t0, n: xs[:, kc, t0:t0 + n], t_xs, blocks, rstd, t_rstd, sq_ring, D)
            for (t0, n) in blocks:
                col = 1 if t0 >= TL else 0
                for kc in range(KC):
                    tmp, tt = tmp_ring.next()
                    k.op("dve", lambda e: e.scalar_tensor_tensor(out=tmp[:, :n], in0=xs[:, kc, t0:t0 + n], scalar=coefA[:, i, kc, col:col + 1],
                                                                  in1=rstd[:, t0:t0 + n], op0=ALU.mult, op1=ALU.mult),
                         R=[t_xs, t_coef, t_rstd], W=[tt])
                    k.opw("act", lambda e: e.activation(out=u[:, kc, t0:t0 + n], in_=tmp[:, :n], func=AF.Identity,
                                                        bias=coefS[:, i, kc, col:col + 1], scale=1.0), R=[tt, t_coef], W=[t_u], keep=[t_u])

        def ffn(f, u, t_u, blocks, out, t_o, ph):
            act = k.sb([128, HG, T], BF16, ph)
            t_act = Trk()
            sring = mkring(k, 6, [128, 512], F32, ph)
            st = {}
            for gi in range(NH):
                def epi_in(cc, bi, t0, n, ps, ptr):
                    h = (cc % (2 * HG)) // 2
                    if cc % 2 == 0:
                        s_, str_ = sring.next()
                        k.op("act", lambda e: e.activation(out=s_[:, :n], in_=ps[:, :n], func=AF.Silu), R=[ptr], W=[str_])
                        st[bi] = (s_, str_)
                    else:
                        s_, str_ = st[bi]
                        k.opw("dve", lambda e: e.tensor_tensor(out=act[:, h, t0:t0 + n], in0=s_[:, :n], in1=ps[:, :n], op=ALU.mult),
                              R=[str_, ptr], W=[t_act], keep=[t_act])
                lin(ffn_in[f], range(gi * 2 * HG, (gi + 1) * 2 * HG), KC, lambda kc, t0, n: (u[:, kc, t0:t0 + n], t_u), blocks, epi_in)

                def epi_out(cc, bi, t0, n, ps, ptr):
                    oc = cc % KC
                    if gi == 0:
                        k.opw("act", lambda e: e.activation(out=out[:, oc, t0:t0 + n], in_=ps[:, :n], func=AF.Identity), R=[ptr], W=[t_o], keep=[t_o])
                    else:
                        k.opw("dve", lambda e: e.tensor_tensor(out=out[:, oc, t0:t0 + n], in0=out[:, oc, t0:t0 + n], in1=ps[:, :n], op=ALU.add),
                              R=[ptr, t_o], W=[t_o], keep=[t_o])
                lin(ffn_out[f], range(gi * KC, (gi + 1) * KC), HG, lambda kc, t0, n: (act[:, kc, t0:t0 + n], t_act), blocks, epi_out)

        def ada_post(i, out, t_o, blocks, xsrc, rstd, t_rstd, sq_ring, tmp_ring, xring, sink):
            rstd_of(lambda kc, t0, n: out[:, kc, t0:t0 + n], t_o, blocks, rstd, t_rstd, sq_ring, D)
            for (t0, n) in blocks:
                col = 1 if t0 >= TL else 0
                for kc in range(KC):
                    xt, xtr = xring.next()
                    k.op("sp", lambda e: e.dma_start(out=xt[:, :n], in_=xsrc[0][:, kc, t0:t0 + n]), R=[xsrc[1]], W=[xtr], dsem=xtr)
                    tmp, tt = tmp_ring.next()
                    k.op("dve", lambda e: e.scalar_tensor_tensor(out=tmp[:, :n], in0=out[:, kc, t0:t0 + n], scalar=coefG[:, i, kc, col:col + 1],
                                                                  in1=rstd[:, t0:t0 + n], op0=ALU.mult, op1=ALU.mult),
                         R=[t_o, t_coef, t_rstd], W=[tt])
                    k.opw("dve", lambda e: e.tensor_tensor(out=out[:, kc, t0:t0 + n], in0=tmp[:, :n], in1=xt[:, :n], op=ALU.add),
                          R=[tt, xtr, t_o], W=[t_o], keep=[t_o])
                    if sink is not None:
                        sink(kc, t0, n)

        cm = k.sb([128, 8, 128], F32)
        cvec = k.sb([128, 8], F32)
        seljt = k.sb([128, 4], F32)
        t_cm = Trk()
        k.op("sp", lambda e: e.dma_start(out=cm[:, :, :], in_=cmat_d[:, :, :]), W=[t_cm], dsem=t_cm)
        t_cv = Trk()
        k.op("sp", lambda e: e.dma_start(out=cvec[:, :], in_=cvec_d[:, :]), W=[t_cv], dsem=t_cv)
        t_sj = Trk()
        k.op("sp", lambda e: e.dma_start(out=seljt[:, :], in_=selj_d[:, :]), W=[t_sj], dsem=t_sj)
        ident = cm[:, 0, :]
        identb_p = k.sb([128, 128], BF16)
        t_idbp = Trk()
        k.op("dve", lambda e: e.tensor_copy(out=identb_p[:, :], in_=cm[:, 0, :]), R=[t_cm], W=[t_idbp])
        A2 = [(k.sb([128, 2, 32], F32), k.sb([128, 2, 32], F32)) for _ in range(8)]
        t_A2 = Trk()

        def s5_setup():
            with ExitStack() as ph:
                ts = Trk()

                def V(fn, extra=()):
                    k.op("dve", fn, R=[ts] + list(extra), W=[ts])

                def A(fn, extra=()):
                    k.op("act", fn, R=[ts] + list(extra), W=[ts])

                def t3():
                    return k.sb([128, 2, 32], F32, ph)[:, :, :]
                lam = k.sb([128, 3, 2, 32], F32, ph)
                bt = k.sb([128, 2, 2, 32, 16], F32, ph)
                ct = k.sb([128, 2, 2, 32, 16], F32, ph)
                drep = k.sb([128, 64], F32, ph)
                k.op("sp", lambda e: e.dma_start(out=lam[:, :, :, :], in_=s5lam_d[:, :, :, :]), W=[ts], dsem=ts)
                k.opw("sp", lambda e: e.dma_start(out=bt[:, :, :, :, :], in_=s5b_d[:, :, :, :, :]), W=[ts], dsem=ts, keep=[ts])
                k.opw("sp", lambda e: e.dma_start(out=ct[:, :, :, :, :], in_=s5c_d[:, :, :, :, :]), W=[ts], dsem=ts, keep=[ts])
                k.opw("sp", lambda e: e.dma_start(out=drep[:, :], in_=s5d_d[:, :]), W=[ts], dsem=ts, keep=[ts])
                lre, lim = lam[:, 0], lam[:, 1]
                step, lr, ang, sn, cs, t1, t2, mag, are, aim = [t3() for _ in range(10)]
                A(lambda e: e.activation(out=step, in_=lam[:, 2], func=AF.Exp))
                V(lambda e: e.tensor_tensor(out=lr, in0=lre, in1=step, op=ALU.mult))
                V(lambda e: e.tensor_tensor(out=ang, in0=lim, in1=step, op=ALU.mult))
                A(lambda e: e.activation(out=sn, in_=ang, func=AF.Sin, scale=1.0 / 16.0), [t_cv])
                A(lambda e: e.activation(out=cs, in_=ang, func=AF.Sin, scale=1.0 / 16.0, bias=cvec[:, 3:4]), [t_cv])
                for _ in range(4):
                    V(lambda e: e.tensor_tensor(out=t1, in0=cs, in1=cs, op=ALU.mult))
                    V(lambda e: e.tensor_tensor(out=t2, in0=sn, in1=sn, op=ALU.mult))
                    V(lambda e: e.scalar_tensor_tensor(out=sn, in0=cs, scalar=2.0, in1=sn, op0=ALU.mult, op1=ALU.mult))
                    V(lambda e: e.tensor_tensor(out=cs, in0=t1, in1=t2, op=ALU.subtract))
                A(lambda e: e.activation(out=mag, in_=lr, func=AF.Exp))
                V(lambda e: e.tensor_tensor(out=are, in0=mag, in1=cs, op=ALU.mult))
                V(lambda e: e.tensor_tensor(out=aim, in0=mag, in1=sn, op=ALU.mult))

                def cmul(ore, oim, xre, xim, yre, yim):
                    V(lambda e: e.tensor_tensor(out=t1, in0=xre, in1=yre, op=ALU.mult))
                    V(lambda e: e.tensor_tensor(out=t2, in0=xim, in1=yim, op=ALU.mult))
                    V(lambda e: e.tensor_tensor(out=ore, in0=t1, in1=t2, op=ALU.subtract))
                    V(lambda e: e.tensor_tensor(out=t1, in0=xre, in1=yim, op=ALU.mult))
                    V(lambda e: e.tensor_tensor(out=t2, in0=xim, in1=yre, op=ALU.mult))
                    V(lambda e: e.tensor_tensor(out=oim, in0=t1, in1=t2, op=ALU.add))
                pw = [(t3(), t3()) for _ in range(9)]
                V(lambda e: e.memset(pw[0][0], 1.0))
                V(lambda e: e.memset(pw[0][1], 0.0))
                V(lambda e: e.tensor_copy(out=pw[1][0], in_=are))
                V(lambda e: e.tensor_copy(out=pw[1][1], in_=aim))
                for i in range(1, 8):
                    cmul(pw[i + 1][0], pw[i + 1][1], pw[i][0], pw[i][1], are, aim)
                e16, ivre, ivim = t3(), t3(), t3()
                A(lambda e: e.activation(out=e16, in_=lr, func=AF.Exp, scale=-16.0))
                V(lambda e: e.tensor_tensor(out=ivre, in0=pw[8][0], in1=e16, op=ALU.mult))
                V(lambda e: e.scalar_tensor_tensor(out=ivim, in0=pw[8][1], scalar=-1.0, in1=e16, op0=ALU.mult, op1=ALU.mult))
                k.op("dve", lambda e: e.tensor_copy(out=A2[0][0][:, :, :], in_=pw[8][0]), R=[ts], W=[ts, t_A2])
                k.op("dve", lambda e: e.tensor_copy(out=A2[0][1][:, :, :], in_=pw[8][1]), R=[ts], W=[ts, t_A2])
                for i in range(7):
                    cmul(A2[i + 1][0][:, :, :], A2[i + 1][1][:, :, :], A2[i][0][:, :, :], A2[i][1][:, :, :], A2[i][0][:, :, :], A2[i][1][:, :, :])
                k.op("dve", lambda e: e.tensor_copy(out=t1, in_=t1), R=[ts], W=[ts, t_A2])
                den, kre, kim, nre = t3(), t3(), t3(), t3()
                V(lambda e: e.tensor_tensor(out=t1, in0=lre, in1=lre, op=ALU.mult))
                V(lambda e: e.tensor_tensor(out=t2, in0=lim, in1=lim, op=ALU.mult))
                V(lambda e: e.tensor_tensor(out=den, in0=t1, in1=t2, op=ALU.add))
                V(lambda e: e.reciprocal(out=den, in_=den))
                V(lambda e: e.tensor_scalar(out=nre, in0=are, scalar1=-1.0, scalar2=None, op0=ALU.add))
                V(lambda e: e.tensor_tensor(out=t1, in0=nre, in1=lre, op=ALU.mult))
                V(lambda e: e.tensor_tensor(out=t2, in0=aim, in1=lim, op=ALU.mult))
                V(lambda e: e.tensor_tensor(out=t1, in0=t1, in1=t2, op=ALU.add))
                V(lambda e: e.tensor_tensor(out=kre, in0=t1, in1=den, op=ALU.mult))
                V(lambda e: e.tensor_tensor(out=t1, in0=aim, in1=lre, op=ALU.mult))
                V(lambda e: e.tensor_tensor(out=t2, in0=nre, in1=lim, op=ALU.mult))
                V(lambda e: e.tensor_tensor(out=t1, in0=t1, in1=t2, op=ALU.subtract))
                V(lambda e: e.tensor_tensor(out=kim, in0=t1, in1=den, op=ALU.mult))
                bbre = k.sb([128, 2, 32, 16], F32, ph)
                bbim = k.sb([128, 2, 32, 16], F32, ph)
                w1 = k.sb([128, 2, 32, 16], F32, ph)
                w2 = k.sb([128, 2, 32, 16], F32, ph)
                for d in range(2):
                    kr = kre[:, d, :].unsqueeze(2).broadcast_to([128, 32, 16])
                    ki = kim[:, d, :].unsqueeze(2).broadcast_to([128, 32, 16])
                    V(lambda e: e.tensor_tensor(out=w1[:, d], in0=bt[:, 0, d], in1=kr, op=ALU.mult))
                    V(lambda e: e.tensor_tensor(out=w2[:, d], in0=bt[:, 1, d], in1=ki, op=ALU.mult))
                    V(lambda e: e.tensor_tensor(out=bbre[:, d], in0=w1[:, d], in1=w2[:, d], op=ALU.subtract))
                    V(lambda e: e.tensor_tensor(out=w1[:, d], in0=bt[:, 1, d], in1=kr, op=ALU.mult))
                    V(lambda e: e.tensor_tensor(out=w2[:, d], in0=bt[:, 0, d], in1=ki, op=ALU.mult))
                    V(lambda e: e.tensor_tensor(out=bbim[:, d], in0=w1[:, d], in1=w2[:, d], op=ALU.add))
                NQ = 8
                BAre, BAim, CAre, CAim, X1, X2 = [k.sb([128, NQ, 8, 16], F32, ph) for _ in range(6)]
                BAre_b, BAim_b, C8re_b, C8im_b, Wre_b, Wim_b = [k.sb([128, NQ, 128], BF16, ph) for _ in range(6)]
                Gt = k.sb([128, NQ, 2, 2, 64], BF16, ph)
                Tacc = k.sb([128, 2 * NQ, 128], F32, ph)
                Tt = k.sb([128, 2 * NQ, 128], BF16, ph)
                tmpT = k.sb([128, 128], F32, ph)
                Gd_v = Gd.rearrange("p (q m d r x) -> p q m d r x", q=32, m=2, d=2, r=2, x=64)
                Wd_v = Wd.rearrange("p (d r q x) -> p d r q x", d=2, r=2, q=32, x=128)
                Td_v = Td.rearrange("p (g x) -> p g x", g=64, x=128)
                for qc in range(0 if 'q' in SKIP else 32 // NQ):
                    qs = slice(qc * NQ, (qc + 1) * NQ)
                    for d in range(2):
                        for s in range(8):
                            fexp = (7 - s) if d == 0 else s
                            eexp = (s + 1) if d == 0 else (8 - s)
                            for (ore, oim, src_re, src_im, ex) in ((BAre, BAim, bbre, bbim, fexp), (CAre, CAim, ct[:, 0], ct[:, 1], eexp)):
                                pr = pw[ex][0][:, d, qs].unsqueeze(2).broadcast_to([128, NQ, 16])
                                pi = pw[ex][1][:, d, qs].unsqueeze(2).broadcast_to([128, NQ, 16])
                                xre = src_re[:, d, qs, :]
                                xim = src_im[:, d, qs, :]
                                V(lambda e: e.tensor_tensor(out=X1[:, :, s, :], in0=xre, in1=pr, op=ALU.mult))
                                V(lambda e: e.tensor_tensor(out=X2[:, :, s, :], in0=xim, in1=pi, op=ALU.mult))
                                V(lambda e: e.tensor_tensor(out=ore[:, :, s, :], in0=X1[:, :, s, :], in1=X2[:, :, s, :], op=ALU.subtract))
                                V(lambda e: e.tensor_tensor(out=X1[:, :, s, :], in0=xre, in1=pi, op=ALU.mult))
                                V(lambda e: e.tensor_tensor(out=X2[:, :, s, :], in0=xim, in1=pr, op=ALU.mult))
                                V(lambda e: e.tensor_tensor(out=oim[:, :, s, :], in0=X1[:, :, s, :], in1=X2[:, :, s, :], op=ALU.add))
                        fl = lambda tl: tl[:, :, :, :].rearrange("p q s c -> p q (s c)")
                        V(lambda e: e.tensor_copy(out=BAre_b[:, :, :], in_=fl(BAre)))
                        V(lambda e: e.tensor_copy(out=BAim_b[:, :, :], in_=fl(BAim)))
                        V(lambda e: e.tensor_copy(out=Wre_b[:, :, :], in_=fl(CAre)))
                        V(lambda e: e.tensor_scalar(out=Wim_b[:, :, :], in0=fl(CAim), scalar1=-1.0, scalar2=None, op0=ALU.mult))
                        ir = ivre[:, d, qs].unsqueeze(2).broadcast_to([128, NQ, 128])
                        ii = ivim[:, d, qs].unsqueeze(2).broadcast_to([128, NQ, 128])
                        V(lambda e: e.tensor_tensor(out=fl(X1), in0=fl(CAre), in1=ir, op=ALU.mult))
                        V(lambda e: e.tensor_tensor(out=fl(X2), in0=fl(CAim), in1=ii, op=ALU.mult))
                        V(lambda e: e.tensor_tensor(out=C8re_b[:, :, :], in0=fl(X1), in1=fl(X2), op=ALU.subtract))
                        V(lambda e: e.tensor_tensor(out=fl(X1), in0=fl(CAre), in1=ii, op=ALU.mult))
                        V(lambda e: e.tensor_tensor(out=fl(X2), in0=fl(CAim), in1=ir, op=ALU.mult))
                        V(lambda e: e.tensor_tensor(out=fl(X1), in0=fl(X1), in1=fl(X2), op=ALU.add))
                        V(lambda e: e.tensor_scalar(out=C8im_b[:, :, :], in0=fl(X1), scalar1=-1.0, scalar2=None, op0=ALU.mult))
                        k.opw("sp", lambda e: e.dma_start(out=Wd_v[:, d, 0, qs, :], in_=Wre_b[:, :, :]), R=[ts], W=[t_Wd], dsem=ts, keep=[t_Wd])
                        k.opw("sp", lambda e: e.dma_start(out=Wd_v[:, d, 1, qs, :], in_=Wim_b[:, :, :]), R=[ts], W=[t_Wd], dsem=ts, keep=[t_Wd])
                        for ql in range(0 if 'g' in SKIP else NQ):
                            for m in range(2):
                                pst, psttr = psbf.next()
                                for r, src in enumerate((BAre_b, BAim_b)):
                                    k.op("pe", lambda e: e.transpose(pst[:, r * 64:(r + 1) * 64], src[m * 64:(m + 1) * 64, ql, :],
                                                                     identb_p[m * 64:(m + 1) * 64, m * 64:(m + 1) * 64]), R=[ts, t_idbp], W=[psttr])
                                k.op("act", lambda e: e.activation(out=Gt[:, ql, m, :, :].rearrange("p r x -> p (r x)"), in_=pst[:, :128],
                                                                   func=AF.Identity), R=[psttr, ts], W=[ts])
                        for m in range(2):
                            k.opw("sp", lambda e: e.dma_start(out=Gd_v[:, qs, m, d, :, :], in_=Gt[:, :, m, :, :]), R=[ts], W=[t_Gd], dsem=ts, keep=[t_Gd])
                        for ql in range(0 if 't' in SKIP else NQ):
                            for m in range(2):
                                gi = ql * 2 + m
                                ps, ptr = k.psum.next()
                                sl = slice(m * 64, (m + 1) * 64)
                                k.op("pe", lambda e: e.matmul(ps[:, :128], BAre_b[sl, ql, :], C8re_b[sl, ql, :], start=True, stop=False), R=[ts], W=[ptr])
                                k.op("pe", lambda e: e.matmul(ps[:, :128], BAim_b[sl, ql, :], C8im_b[sl, ql, :], start=False, stop=True), R=[ts], W=[ptr])
                                if d == 0:
                                    V(lambda e: e.tensor_tensor(out=Tacc[:, gi, :], in0=ps[:, :128], in1=cm[:, 1, :], op=ALU.mult), [ptr, t_cm])
                                else:
                                    V(lambda e: e.tensor_tensor(out=tmpT[:, :], in0=ps[:, :128], in1=cm[:, 2, :], op=ALU.mult), [ptr, t_cm])
                                    V(lambda e: e.tensor_tensor(out=Tacc[:, gi, :], in0=Tacc[:, gi, :], in1=tmpT[:, :], op=ALU.add))
                    for gi in range(2 * NQ):
                        g_ = qc * 2 * NQ + gi
                        V(lambda e: e.scalar_tensor_tensor(out=Tt[:, gi, :], in0=ident, scalar=drep[:, g_:g_ + 1], in1=Tacc[:, gi, :],
                                                           op0=ALU.mult, op1=ALU.add), [t_cm])
                    k.opw("sp", lambda e: e.dma_start(out=Td_v[:, qc * 2 * NQ:(qc + 1) * 2 * NQ, :], in_=Tt[:, :, :]), R=[ts], W=[t_Td], dsem=ts, keep=[t_Td])
                k.barrier()
        def s5_main():
            with ExitStack() as ph:
                U = k.sb([128, 64, NB], BF16, ph)
                t_U = Trk()
                Hist = k.sb([128, 2, 2, 32, NBL], BF16, ph)
                t_H = Trk()
                with ExitStack() as pa:
                    sT = k.sb([128, 8, T], BF16, pa)
                    selA = k.sb([128, 64, 128], BF16, pa)
                    t_s, t_sel = Trk(), Trk()
                    k.op("sp", lambda e: e.dma_start(out=sT[:, :, :], in_=sTd[:, :, :]), R=[t_sTd], W=[t_s], dsem=t_s)
                    k.op("pool", lambda e: e.dma_start(out=selA[:, :, :], in_=selA_d[:, :, :]), W=[t_sel], dsem=t_sel)
                    for g_ in range(64):
                        ps, ptr = k.psum.next()
                        ch, gl = g_ // 8, g_ % 8
                        for s in range(8):
                            k.op("pe", lambda e: e.matmul(ps[:, :NB], selA[:, gl * 8 + s, :], sT[:, ch, s:T:8], start=(s == 0), stop=(s == 7)),
                                 R=[t_s, t_sel], W=[ptr])
                        k.opw("act", lambda e: e.activation(out=U[:, g_, :], in_=ps[:, :NB], func=AF.Identity), R=[ptr], W=[t_U], keep=[t_U])
                    k.barrier()
                if STAGE == 11:
                    k.op("sp", lambda e: e.dma_start(out=aTd[:, :, 0:NB], in_=U[:, 0:8, :]), R=[t_U], W=[t_aTd], dsem=t_aTd)
                    return
                with ExitStack() as pb:
                    F = k.sb([128, 2, 2, 32, NB], F32, pb)
                    t_F = Trk()
                    Gd_v = Gd.rearrange("p (g d r x) -> p g d r x", g=64, d=2, r=2, x=64)
                    gring = mkring(k, 2, [128, 8, 2, 2, 64], BF16, pb)
                    for gc in range(8):
                        gt, gtr = gring.next()
                        k.op("sp", lambda e: e.dma_start(out=gt[:, :, :, :, :].rearrange("p g d r x -> p g (d r x)"),
                                                         in_=Gd_v[:, gc * 8:(gc + 1) * 8].rearrange("p g d r x -> p g (d r x)")), R=[t_Gd], W=[gtr], dsem=gtr)
                        for ql in range(4):
                            q = gc * 4 + ql
                            for d in range(2):
                                for r in range(2):
                                    ps, ptr = k.psum.next()
                                    for m in range(2):
                                        g_ = 2 * q + m
                                        k.op("pe", lambda e: e.matmul(ps[m * 64:(m + 1) * 64, :NB], gt[:, g_ % 8, d, r, :], U[:, g_, :], start=True, stop=True),
                                             R=[gtr, t_U], W=[ptr])
                                    k.opw("act", lambda e: e.activation(out=F[:, r, d, q, :], in_=ps[:, :NB], func=AF.Identity), R=[ptr], W=[t_F], keep=[t_F])
                    tmp = [[k.sb([128, 32], F32, pb)[:, :] for _ in range(2)] for _ in range(2)]
                    t_rec = [Trk(), Trk()]
                    for d, eng in ((0, "dve"), (1, "pool")):
                        are, aim = A2[0][0][:, d, :], A2[0][1][:, d, :]
                        t1, t2 = tmp[d]
                        if d == 0:
                            order = [(b, b - 1) for b in range(1, NBL)] + [(b, b - 1) for b in range(NBL + 1, NB)]
                        else:
                            order = [(b, b + 1) for b in range(NBL - 2, -1, -1)] + [(b, b + 1) for b in range(NB - 2, NBL - 1, -1)]
                        tr = t_rec[d]
                        first = True
                        for (b, pb_) in order:
                            R0 = [tr, t_A2] + ([t_F] if first else [])
                            first = False
                            pre, pim = F[:, 0, d, :, pb_], F[:, 1, d, :, pb_]
                            cre, cim = F[:, 0, d, :, b], F[:, 1, d, :, b]
                            k.op(eng, lambda e: e.tensor_tensor(out=t1, in0=are, in1=pre, op=ALU.mult), R=R0, W=[tr])
                            k.op(eng, lambda e: e.tensor_tensor(out=t2, in0=aim, in1=pim, op=ALU.mult), R=[tr], W=[tr])
                            k.op(eng, lambda e: e.tensor_tensor(out=t1, in0=t1, in1=t2, op=ALU.subtract), R=[tr], W=[tr])
                            k.op(eng, lambda e: e.tensor_tensor(out=t2, in0=are, in1=pim, op=ALU.mult), R=[tr], W=[tr])
                            k.op(eng, lambda e: e.tensor_tensor(out=cim, in0=cim, in1=t2, op=ALU.add), R=[tr], W=[tr])
                            k.op(eng, lambda e: e.tensor_tensor(out=t2, in0=aim, in1=pre, op=ALU.mult), R=[tr], W=[tr])
                            k.op(eng, lambda e: e.tensor_tensor(out=cim, in0=cim, in1=t2, op=ALU.add), R=[tr], W=[tr])
                            k.op(eng, lambda e: e.tensor_tensor(out=cre, in0=cre, in1=t1, op=ALU.add), R=[tr], W=[tr])
                    pub = k.sb([128, 2, 2, 2, 32], F32, pb)
                    t_pub = Trk()
                    fin = {(0, 0): NBL - 1, (1, 0): NB - 1, (0, 1): 0, (1, 1): NBL}
                    for (pc, d), b in fin.items():
                        for r in range(2):
                            k.opw("act", lambda e: e.activation(out=pub[:, pc, d, r, :], in_=F[:, r, d, :, b], func=AF.Identity),
                                  R=[t_rec[0], t_rec[1], t_F], W=[t_pub], keep=[t_pub])
                    k.op("sp", lambda e: e.dma_start(out=pub5[:, :], in_=pub[:, :, :, :, :].rearrange("p a b c q -> p (a b c q)")), R=[t_pub], W=[t_pub5], dsem=t_pub5)
                    k.op("pool", lambda e: e.collective_compute("AllGather", ALU.bypass, replica_groups=[[0, 1, 2, 3], [4, 5, 6, 7]],
                                                                 ins=[pub5_t.ap().opt()], outs=[gat5_t.ap().opt()]), R=[t_pub5], W=[t_gat5], cc=True)
                    gat = k.sb([128, 4, 2, 2, 2, 32], F32, pb)
                    t_gat = Trk()
                    k.op("sp", lambda e: e.dma_start(out=gat[:, :, :, :, :, :].rearrange("p k a b c q -> p k (a b c q)"),
                                                     in_=gat5.rearrange("(k p) f -> p k f", p=128)), R=[t_gat5], W=[t_gat], dsem=t_gat)
                    Hin = k.sb([128, 2, 2, 32], F32, pb)
                    hc = [k.sb([128, 32], F32, pb)[:, :] for _ in range(6)]
                    tc = Trk()

                    def Vc(fn, extra=()):
                        k.op("dve", fn, R=[tc] + list(extra), W=[tc])

                    def cmuladd(hre, him, are_, aim_, fre, fim):
                        u1, u2, u3, u4 = hc[2:6]
                        Vc(lambda e: e.tensor_tensor(out=u1, in0=are_, in1=hre, op=ALU.mult), [t_A2])
                        Vc(lambda e: e.tensor_tensor(out=u2, in0=aim_, in1=him, op=ALU.mult))
                        Vc(lambda e: e.tensor_tensor(out=u3, in0=are_, in1=him, op=ALU.mult))
                        Vc(lambda e: e.tensor_tensor(out=u4, in0=aim_, in1=hre, op=ALU.mult))
                        Vc(lambda e: e.tensor_tensor(out=u1, in0=u1, in1=u2, op=ALU.subtract))
                        Vc(lambda e: e.tensor_tensor(out=hre, in0=u1, in1=fre, op=ALU.add), [t_gat])
                        Vc(lambda e: e.tensor_tensor(out=u3, in0=u3, in1=u4, op=ALU.add))
                        Vc(lambda e: e.tensor_tensor(out=him, in0=u3, in1=fim, op=ALU.add), [t_gat])
                    for d in range(2):
                        hre, him = hc[0], hc[1]
                        a64 = (A2[3][0][:, d, :], A2[3][1][:, d, :])
                        a1k = (A2[7][0][:, d, :], A2[7][1][:, d, :])
                        Vc(lambda e: e.memset(hre, 0.0))
                        Vc(lambda e: e.memset(him, 0.0))
                        cord = [0, 1, 2, 3] if d == 0 else [3, 2, 1, 0]
                        for i in cord:
                            cmuladd(hre, him, a64[0], a64[1], gat[:, i, 1, d, 0, :], gat[:, i, 1, d, 1, :])
                        xord = [0, 1, 2] if d == 0 else [3, 2, 1]
                        first_sel = 0 if d == 0 else 3
                        Vc(lambda e: e.tensor_scalar(out=Hin[:, d, 0, :], in0=hre, scalar1=seljt[:, first_sel:first_sel + 1], scalar2=None, op0=ALU.mult), [t_sj])
                        Vc(lambda e: e.tensor_scalar(out=Hin[:, d, 1, :], in0=him, scalar1=seljt[:, first_sel:first_sel + 1], scalar2=None, op0=ALU.mult))
                        for i in xord:
                            cmuladd(hre, him, a1k[0], a1k[1], gat[:, i, 0, d, 0, :], gat[:, i, 0, d, 1, :])
                            sidx = i + 1 if d == 0 else i - 1
                            Vc(lambda e: e.scalar_tensor_tensor(out=Hin[:, d, 0, :], in0=hre, scalar=seljt[:, sidx:sidx + 1], in1=Hin[:, d, 0, :], op0=ALU.mult, op1=ALU.add))
                            Vc(lambda e: e.scalar_tensor_tensor(out=Hin[:, d, 1, :], in0=him, scalar=seljt[:, sidx:sidx + 1], in1=Hin[:, d, 1, :], op0=ALU.mult, op1=ALU.add))
                    Hc = k.sb([128, 2, 16, NBL], F32, pb)
                    hx = [k.sb([128, 16, 64], F32, pb) for _ in range(2)]
                    for d in range(2):
                        for qh in range(2):
                            qs = slice(qh * 16, (qh + 1) * 16)
                            pos0 = 0 if d == 0 else NBL - 1
                            Vc(lambda e: e.tensor_copy(out=Hc[:, 0, :, pos0], in_=Hin[:, d, 0, qs]))
                            Vc(lambda e: e.tensor_copy(out=Hc[:, 1, :, pos0], in_=Hin[:, d, 1, qs]))
                            for lv in range(7):
                                n = 1 << lv
                                if d == 0:
                                    src, dst = slice(0, n), slice(n, 2 * n)
                                else:
                                    src, dst = slice(NBL - n, NBL), slice(NBL - 2 * n, NBL - n)
                                ar = A2[lv][0][:, d, qs].unsqueeze(2).broadcast_to([128, 16, n])
                                ai = A2[lv][1][:, d, qs].unsqueeze(2).broadcast_to([128, 16, n])
                                x1, x2 = hx[0][:, :, :n], hx[1][:, :, :n]
                                sre, sim = Hc[:, 0, :, src], Hc[:, 1, :, src]
                                Vc(lambda e: e.tensor_tensor(out=x1, in0=sre, in1=ar, op=ALU.mult), [t_A2])
                                Vc(lambda e: e.tensor_tensor(out=x2, in0=sim, in1=ai, op=ALU.mult))
                                Vc(lambda e: e.tensor_tensor(out=Hc[:, 0, :, dst], in0=x1, in1=x2, op=ALU.subtract))
                                Vc(lambda e: e.tensor_tensor(out=x1, in0=sre, in1=ai, op=ALU.mult))
                                Vc(lambda e: e.tensor_tensor(out=x2, in0=sim, in1=ar, op=ALU.mult))
                                Vc(lambda e: e.tensor_tensor(out=Hc[:, 1, :, dst], in0=x1, in1=x2, op=ALU.add))
                            for r in range(2):
                                if d == 0:
                                    k.opw("dve", lambda e: e.tensor_copy(out=Hist[:, d, r, qs, 0:1], in_=Hc[:, r, :, 0:1]), R=[tc], W=[t_H], keep=[t_H])
                                    k.opw("dve", lambda e: e.tensor_tensor(out=Hist[:, d, r, qs, 1:NBL], in0=Hc[:, r, :, 1:NBL], in1=F[:, r, d, qs, 0:NBL - 1], op=ALU.add),
                                          R=[tc, t_rec[0], t_rec[1], t_F], W=[t_H], keep=[t_H])
                                else:
                                    k.opw("dve", lambda e: e.tensor_copy(out=Hist[:, d, r, qs, NBL - 1:NBL], in_=Hc[:, r, :, NBL - 1:NBL]), R=[tc], W=[t_H], keep=[t_H])
                                    k.opw("dve", lambda e: e.tensor_tensor(out=Hist[:, d, r, qs, 0:NBL - 1], in0=Hc[:, r, :, 0:NBL - 1], in1=F[:, r, d, qs, 1:NBL], op=ALU.add),
                                          R=[tc, t_rec[0], t_rec[1], t_F], W=[t_H], keep=[t_H])
                    k.barrier()
                with ExitStack() as pc_:
                    selB = k.sb([128, 64, 128], BF16, pc_)
                    t_selB = Trk()
                    k.op("pool", lambda e: e.dma_start(out=selB[:, :, :], in_=selB_d[:, :, :]), W=[t_selB], dsem=t_selB)
                    Td_v = Td.rearrange("p (g x) -> p g x", g=64, x=128)
                    Wd_v = Wd.rearrange("p (d r q x) -> p d r q x", d=2, r=2, q=32, x=128)
                    tring = mkring(k, 2, [128, 8, 128], BF16, pc_)
                    wring2 = mkring(k, 2, [128, 2, 2, 4, 128], BF16, pc_)
                    yring = mkring(k, 16, [128, 128], BF16, pc_)
                    gring2 = mkring(k, 4, [128, 3, 128], F32, pc_)
                    aring = mkring(k, 2, [128, TL], BF16, pc_)
                    for ch in range(8):
                        tt_, ttr = tring.next()
                        ww_, wtr_ = wring2.next()
                        k.op("sp", lambda e: e.dma_start(out=tt_[:, :, :], in_=Td_v[:, ch * 8:(ch + 1) * 8, :]), R=[t_Td], W=[ttr], dsem=ttr)
                        for d in range(2):
                            for r in range(2):
                                k.opw("sp", lambda e: e.dma_start(out=ww_[:, d, r, :, :], in_=Wd_v[:, d, r, ch * 4:(ch + 1) * 4, :]), R=[t_Wd], W=[wtr_], dsem=wtr_, keep=[wtr_])
                        ys = []
                        for gl in range(8):
                            g_ = ch * 8 + gl
                            q, m = g_ // 2, g_ % 2
                            sl = slice(m * 64, (m + 1) * 64)
                            ps, ptr = k.psum.next()
                            k.op("pe", lambda e: e.matmul(ps[:, :NBL], tt_[:, gl, :], U[:, g_, 0:NBL], start=True, stop=False), R=[ttr, t_U], W=[ptr])
                            for d in range(2):
                                for r in range(2):
                                    k.op("pe", lambda e: e.matmul(ps[:, :NBL], ww_[sl, d, r, q % 4, :], Hist[sl, d, r, q, :], start=False, stop=(d == 1 and r == 1)),
                                         R=[wtr_, t_H], W=[ptr])
                            gg, ggt = gring2.next()
                            k.op("act", lambda e: e.activation(out=gg[:, 0, :], in_=ps[:, :NBL], func=AF.Square), R=[ptr], W=[ggt])
                            k.op("dve", lambda e: e.tensor_scalar(out=gg[:, 0, :], in0=gg[:, 0, :], scalar1=0.044715, scalar2=1.0, op0=ALU.mult, op1=ALU.add), R=[ggt], W=[ggt])
                            k.op("dve", lambda e: e.tensor_tensor(out=gg[:, 1, :], in0=gg[:, 0, :], in1=ps[:, :NBL], op=ALU.mult), R=[ggt, ptr], W=[ggt])
                            k.op("act", lambda e: e.activation(out=gg[:, 2, :], in_=gg[:, 1, :], func=AF.Sigmoid, scale=1.5957691216), R=[ggt], W=[ggt])
                            yt, ytr = yring.next()
                            k.op("dve", lambda e: e.tensor_tensor(out=yt[:, :], in0=gg[:, 2, :], in1=ps[:, :NBL], op=ALU.mult), R=[ggt, ptr], W=[ytr])
                            ys.append((yt, ytr))
                        at, atr = aring.next()
                        for t_ in range(8):
                            ps, ptr = k.psum.next()
                            for gl in range(8):
                                k.op("pe", lambda e: e.matmul(ps[:, :NBL], selB[:, gl * 8 + t_, :], ys[gl][0][:, :], start=(gl == 0), stop=(gl == 7)),
                                     R=[t_selB, ys[gl][1]], W=[ptr])
                            k.opw("act", lambda e: e.activation(out=at[:, t_:TL:8], in_=ps[:, :NBL], func=AF.Identity), R=[ptr], W=[atr], keep=[atr])
                        k.opw("sp", lambda e: e.dma_start(out=aTd[:, ch, :], in_=at[:, :]), R=[atr], W=[t_aTd], dsem=atr, keep=[t_aTd])
                    k.barrier()
        def ret_main():
            with ExitStack() as ph:
                tr_ = Trk()

                def Vr(fn, extra=()):
                    k.op("dve", fn, R=[tr_] + list(extra), W=[tr_])

                def Ar(fn, extra=()):
                    k.op("act", fn, R=[tr_] + list(extra), W=[tr_])
                dl = k.sb([128, 16], F32, ph)
                z = k.sb([128, 16], F32, ph)
                lg = k.sb([128, 16], F32, ph)
                k.op("sp", lambda e: e.dma_start(out=dl[:, :], in_=decay_d[:, :]), W=[tr_], dsem=tr_)
                Ar(lambda e: e.activation(out=z[:, :], in_=dl[:, :], func=AF.Exp, scale=-1.0))
                NT = 12
                Vr(lambda e: e.memset(lg[:, :], ((-1.0) ** (NT + 1)) / NT))
                for n_ in range(NT - 1, 0, -1):
                    Vr(lambda e: e.tensor_tensor(out=lg[:, :], in0=lg[:, :], in1=z[:, :], op=ALU.mult))
                    Vr(lambda e: e.tensor_scalar(out=lg[:, :], in0=lg[:, :], scalar1=((-1.0) ** (n_ + 1)) / n_, scalar2=None, op0=ALU.add))
                Vr(lambda e: e.tensor_tensor(out=lg[:, :], in0=lg[:, :], in1=z[:, :], op=ALU.mult))
                Vr(lambda e: e.tensor_scalar(out=lg[:, :], in0=lg[:, :], scalar1=-1.0, scalar2=None, op0=ALU.mult))
                g128, g64, g1k = [k.sb([128, 16], F32, ph) for _ in range(3)]
                for tl_, sc_ in ((g128, 128.0), (g64, 64.0), (g1k, 1024.0)):
                    Ar(lambda e: e.activation(out=tl_[:, :], in_=lg[:, :], func=AF.Exp, scale=sc_))
                wend = k.sb([128, 2, 16], F32, ph)
                for v_, colf in ((0, 0), (1, 2)):
                    Ar(lambda e: e.activation(out=wend[:, v_, 0:8], in_=lg[:, 0:8], func=AF.Exp, scale=cvec[:, colf:colf + 1]), [t_cv])
                    Ar(lambda e: e.activation(out=wend[:, v_, 8:16], in_=lg[:, 8:16], func=AF.Exp, scale=cvec[:, 1:2]), [t_cv])
                DcT = k.sb([128, 8, 128], BF16, ph)
                win = k.sb([128, 2, 8, 128], F32, ph)
                d1 = k.sb([128, 128], F32, ph)
                d2 = k.sb([128, 128], F32, ph)
                for h in range(8):
                    Ar(lambda e: e.activation(out=d1[:, :], in_=cm[:, 3, :], func=AF.Exp, scale=lg[:, h:h + 1]), [t_cm])
                    Vr(lambda e: e.tensor_tensor(out=d1[:, :], in0=d1[:, :], in1=cm[:, 4, :], op=ALU.mult))
                    Ar(lambda e: e.activation(out=d2[:, :], in_=cm[:, 5, :], func=AF.Exp, scale=lg[:, 8 + h:9 + h]))
                    Vr(lambda e: e.tensor_tensor(out=d2[:, :], in0=d2[:, :], in1=cm[:, 6, :], op=ALU.mult))
                    Vr(lambda e: e.tensor_tensor(out=DcT[:, h, :], in0=d1[:, :], in1=d2[:, :], op=ALU.add))
                    Ar(lambda e: e.activation(out=win[:, 0, h, :], in_=cm[:, 7, :], func=AF.Exp, scale=lg[:, h:h + 1]))
                    Vr(lambda e: e.tensor_scalar(out=d1[:, :], in0=cm[:, 7, :], scalar1=-1.0, scalar2=129.0, op0=ALU.mult, op1=ALU.add))
                    Ar(lambda e: e.activation(out=win[:, 1, h, :], in_=d1[:, :], func=AF.Exp, scale=lg[:, 8 + h:9 + h]))
                ktok_v = ktokd.rearrange("c p (h x) -> p c h x", h=8, x=128)
                vtok_v = vtokd.rearrange("c p (h x) -> p c h x", h=8, x=256)
                kt = k.sb([128, 9, 128], BF16, ph)
                vt = k.sb([128, 9, 256], BF16, ph)
                t_kv = Trk()
                kw = mkring(k, 3, [128, 128], BF16, ph)
                S = [k.sb([128, 256], F32, ph) for _ in range(2)]
                pubR_t = k.sb([128, 2, 2, 8, 256], F32, ph)
                t_pubt = Trk()

                def load_head(h):
                    k.op("sp", lambda e: e.dma_start(out=kt[:, :, :], in_=ktok_v[:, :, h, :]), R=[t_ktok], W=[t_kv], dsem=t_kv)
                    k.opw("sp", lambda e: e.dma_start(out=vt[:, :, :], in_=vtok_v[:, :, h, :]), R=[t_vtok], W=[t_kv], dsem=t_kv, keep=[t_kv])

                def kv_chunk(h, d, c, ps, ptr):
                    rows = 64 if c == 8 else 128
                    kwt, kwtr = kw.next()
                    k.op("dve", lambda e: e.tensor_scalar(out=kwt[:rows, :], in0=kt[:rows, c, :], scalar1=wend[:rows, 1 if c == 8 else 0, d * 8 + h:d * 8 + h + 1],
                                                          scalar2=None, op0=ALU.mult), R=[t_kv, tr_], W=[kwtr])
                    k.op("pe", lambda e: e.matmul(ps[:, :256], kwt[:rows, :], vt[:rows, c, :], start=True, stop=True), R=[kwtr, t_kv], W=[ptr])

                if STAGE == 30:
                    k.barrier()
                    return
                for h in range(8):
                    load_head(h)
                    for d in range(2):
                        order = list(range(8)) if d == 0 else list(range(7, -1, -1))
                        for ci, c in enumerate(order):
                            ps, ptr = k.psum.next()
                            kv_chunk(h, d, c, ps, ptr)
                            dst = pubR_t[:, 0, d, h, :]
                            if ci == 0:
                                k.opw("act", lambda e: e.activation(out=dst, in_=ps[:, :256], func=AF.Identity), R=[ptr], W=[t_pubt], keep=[t_pubt])
                            else:
                                k.opw("dve", lambda e: e.scalar_tensor_tensor(out=dst, in0=dst, scalar=g128[:, d * 8 + h:d * 8 + h + 1], in1=ps[:, :256],
                                                                               op0=ALU.mult, op1=ALU.add), R=[ptr, t_pubt, tr_], W=[t_pubt], keep=[t_pubt])
                        ps, ptr = k.psum.next()
                        kv_chunk(h, d, 8, ps, ptr)
                        k.opw("act", lambda e: e.activation(out=pubR_t[:, 1, d, h, :], in_=ps[:, :256], func=AF.Identity), R=[ptr], W=[t_pubt], keep=[t_pubt])
                for h in range(8):
                    k.opw("sp", lambda e: e.dma_start(out=pubRt_[h].ap().rearrange("p (a d x) -> p a d x", a=2, d=2, x=256), in_=pubR_t[:, :, :, h, :]),
                          R=[t_pubt], W=[t_pubR], dsem=t_pubR, keep=[t_pubR])
                if STAGE == 31:
                    k.barrier()
                    return
                for h in range(8):
                    k.opw("pool", lambda e: e.collective_compute("AllGather", ALU.bypass, replica_groups=[[0, 1, 2, 3], [4, 5, 6, 7]],
                                                                  ins=[pubRt_[h].ap().opt()], outs=[gatRt_[h].ap().opt()]), R=[t_pubR], W=[t_gatR], keep=[t_gatR], cc=True)
                if STAGE == 32:
                    k.barrier()
                    return
                gat_v = [gatRt_[h].ap().rearrange("(k p) (a d x) -> p k a d x", p=128, a=2, d=2, x=256) for h in range(8)]
                Hin = [k.sb([128, 8, 256], F32, ph) for _ in range(2)]
                Hc_ = k.sb([128, 8, 256], F32, ph)
                gring = mkring(k, 2, [128, 8, 256], F32, ph)
                t_hin = Trk()
                for d in range(2):
                    cord = [0, 1, 2, 3] if d == 0 else [3, 2, 1, 0]
                    for n_, i in enumerate(cord):
                        gt_, gtr_ = gring.next()
                        for h in range(8):
                            k.opw("sp", lambda e: e.dma_start(out=gt_[:, h, :], in_=gat_v[h][:, i, 1, d, :]), R=[t_gatR], W=[gtr_], dsem=gtr_, keep=[gtr_])
                        for h in range(8):
                            if n_ == 0:
                                k.op("dve", lambda e: e.tensor_copy(out=Hc_[:, h, :], in_=gt_[:, h, :]), R=[gtr_, t_hin], W=[t_hin])
                            else:
                                k.op("dve", lambda e: e.scalar_tensor_tensor(out=Hc_[:, h, :], in0=Hc_[:, h, :], scalar=g64[:, d * 8 + h:d * 8 + h + 1], in1=gt_[:, h, :],
                                                                              op0=ALU.mult, op1=ALU.add), R=[gtr_, t_hin, tr_], W=[t_hin])
                    fs = 0 if d == 0 else 3
                    k.op("dve", lambda e: e.tensor_scalar(out=Hin[d][:, :, :], in0=Hc_[:, :, :], scalar1=seljt[:, fs:fs + 1], scalar2=None, op0=ALU.mult), R=[t_hin, t_sj], W=[t_hin])
                    xord = [0, 1, 2] if d == 0 else [3, 2, 1]
                    for i in xord:
                        gt_, gtr_ = gring.next()
                        for h in range(8):
                            k.opw("sp", lambda e: e.dma_start(out=gt_[:, h, :], in_=gat_v[h][:, i, 0, d, :]), R=[t_gatR], W=[gtr_], dsem=gtr_, keep=[gtr_])
                        for h in range(8):
                            k.op("dve", lambda e: e.scalar_tensor_tensor(out=Hc_[:, h, :], in0=Hc_[:, h, :], scalar=g1k[:, d * 8 + h:d * 8 + h + 1], in1=gt_[:, h, :],
                                                                          op0=ALU.mult, op1=ALU.add), R=[gtr_, t_hin, tr_], W=[t_hin])
                        si = i + 1 if d == 0 else i - 1
                        k.op("dve", lambda e: e.scalar_tensor_tensor(out=Hin[d][:, :, :], in0=Hc_[:, :, :], scalar=seljt[:, si:si + 1], in1=Hin[d][:, :, :],
                                                                      op0=ALU.mult, op1=ALU.add), R=[t_hin, t_sj], W=[t_hin])
                qTh = k.sb([128, TL], BF16, ph)
                kTh = k.sb([128, TL], BF16, ph)
                t_qk = Trk()
                Sb = k.sb([128, 8, 256], BF16, ph)
                t_Sb = Trk()
                Sfb = mkring(k, 2, [128, 256], BF16, ph)
                pmr = mkring(k, 2, [128, 128], BF16, ph)
                qwr = mkring(k, 4, [128, 128], BF16, ph)
                ofr = mkring(k, 4, [128, 128], F32, ph)
                sqr = mkring(k, 4, [128, 128], BF16, ph)
                rsr = mkring(k, 2, [128, 128], F32, ph)
                otr = mkring(k, 2, [128, 2, TL], BF16, ph)
                t_S = Trk()
                for h in range(8):
                    load_head(h)
                    k.op("sp", lambda e: e.dma_start(out=qTh[:, :], in_=qTd[:, h, :]), R=[t_qTd], W=[t_qk], dsem=t_qk)
                    k.opw("sp", lambda e: e.dma_start(out=kTh[:, :], in_=kTd[:, h, 0:TL]), R=[t_kTd], W=[t_qk], dsem=t_qk, keep=[t_qk])
                    ot, ottr = otr.next()
                    k.op("dve", lambda e: e.tensor_copy(out=S[1][:, :], in_=Hin[1][:, h, :]), R=[t_hin, t_S], W=[t_S])
                    for c in range(7, -1, -1):
                        k.opw("act", lambda e: e.activation(out=Sb[:, c, :], in_=S[1][:, :], func=AF.Identity), R=[t_S], W=[t_Sb], keep=[t_Sb])
                        if c > 0:
                            ps, ptr = k.psum.next()
                            kv_chunk(h, 1, c, ps, ptr)
                            k.op("dve", lambda e: e.scalar_tensor_tensor(out=S[1][:, :], in0=S[1][:, :], scalar=g128[:, 8 + h:9 + h], in1=ps[:, :256],
                                                                          op0=ALU.mult, op1=ALU.add), R=[ptr, t_S, tr_], W=[t_S])
                    k.op("dve", lambda e: e.tensor_copy(out=S[0][:, :], in_=Hin[0][:, h, :]), R=[t_hin, t_S], W=[t_S])
                    for c in range(8):
                        cs_ = slice(c * 128, (c + 1) * 128)
                        sf, sftr = Sfb.next()
                        k.op("act", lambda e: e.activation(out=sf[:, :], in_=S[0][:, :], func=AF.Identity), R=[t_S], W=[sftr])
                        ps, ptr = k.psum.next()
                        k.op("pe", lambda e: e.matmul(ps[:, :128], kTh[:, cs_], qTh[:, cs_], start=True, stop=True), R=[t_qk], W=[ptr])
                        pm, pmtr = pmr.next()
                        k.op("dve", lambda e: e.tensor_tensor(out=pm[:, :], in0=ps[:, :128], in1=DcT[:, h, :], op=ALU.mult), R=[ptr, tr_], W=[pmtr])
                        qf, qftr = qwr.next()
                        k.op("dve", lambda e: e.tensor_tensor(out=qf[:, :], in0=qTh[:, cs_], in1=win[:, 0, h, :], op=ALU.mult), R=[t_qk, tr_], W=[qftr])
                        qb, qbtr = qwr.next()
                        k.op("dve", lambda e: e.tensor_tensor(out=qb[:, :], in0=qTh[:, cs_], in1=win[:, 1, h, :], op=ALU.mult), R=[t_qk, tr_], W=[qbtr])
                        pss, psstr = k.psum.next()
                        ofs = []
                        for dv in range(2):
                            ds_ = slice(dv * 128, (dv + 1) * 128)
                            po, potr = k.psum.next()
                            k.op("pe", lambda e: e.matmul(po[:, :128], vt[:, c, ds_], pm[:, :], start=True, stop=False), R=[t_kv, pmtr], W=[potr])
                            k.op("pe", lambda e: e.matmul(po[:, :128], sf[:, ds_], qf[:, :], start=False, stop=False), R=[sftr, qftr], W=[potr])
                            k.op("pe", lambda e: e.matmul(po[:, :128], Sb[:, c, ds_], qb[:, :], start=False, stop=True), R=[t_Sb, qbtr], W=[potr])
                            of, oftr = ofr.next()
                            k.op("act", lambda e: e.activation(out=of[:, :], in_=po[:, :128], func=AF.Identity), R=[potr], W=[oftr])
                            sq, sqtr = sqr.next()
                            k.op("act", lambda e: e.activation(out=sq[:, :], in_=po[:, :128], func=AF.Square), R=[potr], W=[sqtr])
                            k.op("pe", lambda e: e.matmul(pss[:, :128], ones_b[:, :], sq[:, :], start=(dv == 0), stop=(dv == 1)), R=[sqtr, t_ones], W=[psstr])
                            ofs.append((of, oftr))
                        rs, rstr = rsr.next()
                        k.op("act", lambda e: e.activation(out=rs[:, :], in_=pss[:, :128], func=AF.Sqrt, scale=1.0 / 256.0, bias=epsb[:, 0:1]), R=[psstr, t_eps], W=[rstr])
                        k.op("dve", lambda e: e.reciprocal(out=rs[:, :], in_=rs[:, :]), R=[rstr], W=[rstr])
                        for dv in range(2):
                            k.opw("dve", lambda e: e.tensor_tensor(out=ot[:, dv, cs_], in0=ofs[dv][0][:, :], in1=rs[:, :], op=ALU.mult),
                                  R=[ofs[dv][1], rstr], W=[ottr], keep=[ottr])
                        if c < 7:
                            ps2, ptr2 = k.psum.next()
                            kv_chunk(h, 0, c, ps2, ptr2)
                            k.op("dve", lambda e: e.scalar_tensor_tensor(out=S[0][:, :], in0=S[0][:, :], scalar=g128[:, h:h + 1], in1=ps2[:, :256],
                                                                          op0=ALU.mult, op1=ALU.add), R=[ptr2, t_S, tr_], W=[t_S])
                    k.opw("sp", lambda e: e.dma_start(out=oTd[:, 2 * h:2 * h + 2, :], in_=ot[:, :, :]), R=[ottr], W=[t_oTd], dsem=ottr, keep=[t_oTd])
                k.barrier()
        rstd = k.sb([128, T], F32)
        t_rstd = Trk()
        sq_ring = mkring(k, 3, [128, 512], BF16)
        tmp_ring = mkring(k, 3, [128, 512], F32)
        xring = mkring(k, 3, [128, 512], F32)

        if MODE in ("full", "s5", "mix") and STAGE != 9:
            s5_setup()

        if MODE == "full":
            with ExitStack() as ph:
                u = k.sb([128, KC, T], BF16, ph)
                t_u = Trk()
                with ExitStack() as ph2:
                    xs = k.sb([128, KC, T], F32, ph2)
                    t_xs = Trk()
                    for kc in range(KC):
                        k.opw("sp", lambda e: e.dma_start(out=xs[:, kc, :], in_=xT[:, kc, :]), R=[t_in], W=[t_xs], dsem=t_xs, keep=[t_xs])
                    ada_pre(0, xs, t_xs, TB, u, t_u, rstd, t_rstd, sq_ring, tmp_ring)
                    k.barrier()
                out1 = k.sb([128, KC, T], F32, ph)
                t_o1 = Trk()
                ffn(0, u, t_u, TB, out1, t_o1, ph)
                t_o1s = Trk()

                def sink1(kc, t0, n):
                    k.opw("sp", lambda e: e.dma_start(out=x1s[:, kc, t0:t0 + n], in_=out1[:, kc, t0:t0 + n]), R=[t_o1], W=[t_x1s], dsem=t_o1s, keep=[t_x1s])
                ada_post(0, out1, t_o1, TB, (xT, t_in), rstd, t_rstd, sq_ring, tmp_ring, xring, sink1)
                k.barrier()
                ada_pre(1, out1, t_o1, TB, u, t_u, rstd, t_rstd, sq_ring, tmp_ring)
                k.op("sp", lambda e: e.dma_start(out=u2s[:, :, :], in_=u[:, :, :]), R=[t_u], W=[t_u2s], dsem=t_u2s)
                k.barrier()

        if MODE in ("full", "mix"):
            with ExitStack() as ph:
                u = k.sb([128, KC, T], BF16, ph)
                t_u = Trk()
                k.op("sp", lambda e: e.dma_start(out=u[:, :, :], in_=u2s[:, :, :]), R=[t_u2s], W=[t_u], dsem=t_u)
                rhs_u = lambda kc, t0, n: (u[:, kc, t0:t0 + n], t_u)
                stg = mkring(k, 3, [128, T], BF16, ph)
                cur = {}

                def epi_s(cc, bi, t0, n, ps, ptr):
                    if bi == 0:
                        cur["s"] = stg.next()
                    st_, sttr = cur["s"]
                    k.opw("act", lambda e: e.activation(out=st_[:, t0:t0 + n], in_=ps[:, :n], func=AF.Identity), R=[ptr], W=[sttr], keep=[sttr])
                    if bi == 2:
                        k.opw("sp", lambda e: e.dma_start(out=sTd[:, cc, :], in_=st_[:, :]), R=[sttr], W=[t_sTd], dsem=sttr, keep=[t_sTd])
                lin(mix_in, range(0, 8), KC, rhs_u, TB, epi_s)
                rc = k.sb([128, T], F32, ph)
                rs_ = k.sb([128, T], F32, ph)
                pmat = k.sb([128, 128], BF16, ph)
                identb = k.sb([128, 128], BF16, ph)
                t_rope = Trk()
                k.op("sp", lambda e: e.dma_start(out=rc[:, :], in_=ropec_d[:, :]), W=[t_rope], dsem=t_rope)
                k.opw("sp", lambda e: e.dma_start(out=rs_[:, :], in_=ropes_d[:, :]), W=[t_rope], dsem=t_rope, keep=[t_rope])
                t_pm = Trk()
                k.op("pool", lambda e: e.dma_start(out=pmat[:, :], in_=pmat_d[:, :]), W=[t_pm], dsem=t_pm)
                t_idb = Trk()
                k.op("dve", lambda e: e.tensor_copy(out=identb[:, :], in_=cm[:, 0, :]), R=[t_cm], W=[t_idb])
                raw = mkring(k, 3, [128, 512], BF16, ph)
                r1 = mkring(k, 3, [128, 512], F32, ph)
                r2 = mkring(k, 3, [128, 512], F32, ph)
                ktile = mkring(k, 2, [128, 128], BF16, ph)

                def rope_epi(scale, dst, t_dst, is_k):
                    def epi(cc, bi, t0, n, ps, ptr):
                        hh = cc % 8
                        if bi == 0:
                            cur["r"] = stg.next()
                        st_, sttr = cur["r"]
                        rw, rwtr = raw.next()
                        k.op("act", lambda e: e.activation(out=rw[:, :n], in_=ps[:, :n], func=AF.Identity, scale=scale), R=[ptr], W=[rwtr])
                        ps2, ptr2 = k.psum.next()
                        k.op("pe", lambda e: e.matmul(ps2[:, :n], pmat[:, :], rw[:, :n], start=True, stop=True), R=[t_pm, rwtr], W=[ptr2])
                        a1, a1tr = r1.next()
                        a2, a2tr = r2.next()
                        k.op("dve", lambda e: e.tensor_tensor(out=a1[:, :n], in0=rw[:, :n], in1=rc[:, t0:t0 + n], op=ALU.mult), R=[rwtr, t_rope], W=[a1tr])
                        k.op("dve", lambda e: e.tensor_tensor(out=a2[:, :n], in0=ps2[:, :n], in1=rs_[:, t0:t0 + n], op=ALU.mult), R=[ptr2, t_rope], W=[a2tr])
                        k.opw("dve", lambda e: e.tensor_tensor(out=st_[:, t0:t0 + n], in0=a1[:, :n], in1=a2[:, :n], op=ALU.add), R=[a1tr, a2tr], W=[sttr], keep=[sttr])
                        last = 2 if is_k else 1
                        if is_k:
                            for j in range(n // 128 if n >= 128 else 1):
                                rows = min(128, n)
                                tile_i = (t0 // 128) + j
                                pst, psttr = psbf.next()
                                k.op("pe", lambda e: e.transpose(pst[:rows, :128], st_[:, t0 + j * 128:t0 + j * 128 + rows], identb[:, :]), R=[sttr, t_idb], W=[psttr])
                                kt_, kttr = ktile.next()
                                k.op("act", lambda e: e.activation(out=kt_[:rows, :], in_=pst[:rows, :128], func=AF.Identity), R=[psttr], W=[kttr])
                                k.opw("sp", lambda e: e.dma_start(out=ktokd[tile_i, 0:rows, hh * 128:(hh + 1) * 128], in_=kt_[:rows, :]), R=[kttr], W=[t_ktok], dsem=kttr, keep=[t_ktok])
                        if bi == last:
                            ncols = T if is_k else TL
                            k.opw("sp", lambda e: e.dma_start(out=dst[:, hh, 0:ncols], in_=st_[:, 0:ncols]), R=[sttr], W=[t_dst], dsem=sttr, keep=[t_dst])
                    return epi
                if STAGE == 40:
                    k.barrier()
                    return nc
                lin(mix_in, range(8, 16), KC, rhs_u, TBL, rope_epi(128.0 ** -0.5, qTd, t_qTd, False))
                if STAGE == 41:
                    k.barrier()
                    return nc
                lin(mix_in, range(16, 24), KC, rhs_u, TB, rope_epi(1.0, kTd, t_kTd, True))
                if STAGE == 42:
                    k.barrier()
                    return nc
                vtile = mkring(k, 3, [128, 128], BF16, ph)

                def epi_v(cc, bi, t0, n, ps, ptr):
                    vc = cc - 24
                    rw, rwtr = raw.next()
                    k.op("act", lambda e: e.activation(out=rw[:, :n], in_=ps[:, :n], func=AF.Identity), R=[ptr], W=[rwtr])
                    for j in range(n // 128 if n >= 128 else 1):
                        rows = min(128, n)
                        tile_i = (t0 // 128) + j
                        pst, psttr = psbf.next()
                        k.op("pe", lambda e: e.transpose(pst[:rows, :128], rw[:, j * 128:j * 128 + rows], identb[:, :]), R=[rwtr, t_idb], W=[psttr])
                        vt_, vttr = vtile.next()
                        k.op("act", lambda e: e.activation(out=vt_[:rows, :], in_=pst[:rows, :128], func=AF.Identity), R=[psttr], W=[vttr])
                        k.opw("sp", lambda e: e.dma_start(out=vtokd[tile_i, 0:rows, vc * 128:(vc + 1) * 128], in_=vt_[:rows, :]), R=[vttr], W=[t_vtok], dsem=vttr, keep=[t_vtok])
                lin(mix_in, range(24, 40), KC, rhs_u, TB, epi_v)
                k.barrier()
            if STAGE == 43:
                return nc

        if MODE in ("full", "s5", "mix") and STAGE not in (9, 10):
            s5_main()
        if MODE in ("full", "ret", "mix"):
            ret_main()

        if MODE in ("full", "mix"):
            with ExitStack() as pz:
                mb = k.sb([128, KC, TL], BF16, pz)
                t_mb = Trk()
                mring = mkring(k, 3, [128, 512], F32, pz)
                with ExitStack() as px:
                    u = k.sb([128, KC, T], BF16, px)
                    t_u = Trk()
                    k.op("sp", lambda e: e.dma_start(out=u[:, :, :], in_=u2s[:, :, :]), R=[t_u2s], W=[t_u], dsem=t_u)
                    rhs_u = lambda kc, t0, n: (u[:, kc, t0:t0 + n], t_u)
                    sgr = k.sb([128, KC, TL], BF16, px)
                    gated = k.sb([128, KC, TL], BF16, px)
                    t_sgr, t_gt = Trk(), Trk()
                    k.op("sp", lambda e: e.dma_start(out=gated[:, :, :], in_=oTd[:, :, :]), R=[t_oTd], W=[t_gt], dsem=t_gt)
                    sring = mkring(k, 3, [128, 512], F32, px)

                    def epi_gr(cc, bi, t0, n, ps, ptr):
                        i = cc - 72
                        k.opw("act", lambda e: e.activation(out=sgr[:, i, t0:t0 + n], in_=ps[:, :n], func=AF.Sigmoid), R=[ptr], W=[t_sgr], keep=[t_sgr])
                    lin(mix_in, range(72, 88), KC, rhs_u, TBL, epi_gr)

                    def epi_g(cc, bi, t0, n, ps, ptr):
                        i = cc - 40
                        s_, str_ = sring.next()
                        k.op("act", lambda e: e.activation(out=s_[:, :n], in_=ps[:, :n], func=AF.Silu), R=[ptr], W=[str_])
                        k.opw("dve", lambda e: e.tensor_tensor(out=gated[:, i, t0:t0 + n], in0=gated[:, i, t0:t0 + n], in1=s_[:, :n], op=ALU.mult),
                              R=[str_, t_gt], W=[t_gt], keep=[t_gt])
                    lin(mix_in, range(40, 56), KC, rhs_u, TBL, epi_g)

                    def epi_rp(cc, bi, t0, n, ps, ptr):
                        m_, mtr_ = mring.next()
                        k.op("dve", lambda e: e.tensor_tensor(out=m_[:, :n], in0=ps[:, :n], in1=sgr[:, cc, t0:t0 + n], op=ALU.mult), R=[ptr, t_sgr], W=[mtr_])
                        k.opw("sp", lambda e: e.dma_start(out=mrs[:, cc, t0:t0 + n], in_=m_[:, :n]), R=[mtr_], W=[t_mrs], dsem=mtr_, keep=[t_mrs])
                    lin(retp_w, range(16), KC, lambda kc, t0, n: (gated[:, kc, t0:t0 + n], t_gt), TBL, epi_rp)
                    k.barrier()
                with ExitStack() as py:
                    u = k.sb([128, KC, T], BF16, py)
                    t_u = Trk()
                    k.op("sp", lambda e: e.dma_start(out=u[:, :, :], in_=u2s[:, :, :]), R=[t_u2s], W=[t_u], dsem=t_u)
                    rhs_u = lambda kc, t0, n: (u[:, kc, t0:t0 + n], t_u)
                    aT = k.sb([128, 8, TL], BF16, py)
                    t_aT = Trk()
                    k.op("sp", lambda e: e.dma_start(out=aT[:, :, :], in_=aTd[:, :, :]), R=[t_aTd], W=[t_aT], dsem=t_aT)
                    sbr = k.sb([128, KC, TL], BF16, py)
                    t_sbr = Trk()
                    sring = mkring(k, 4, [128, 512], F32, py)
                    st2 = {}

                    def epi_glu(cc, bi, t0, n, ps, ptr):
                        i = cc // 2
                        if cc % 2 == 0:
                            s_, str_ = sring.next()
                            k.op("act", lambda e: e.activation(out=s_[:, :n], in_=ps[:, :n], func=AF.Identity), R=[ptr], W=[str_])
                            st2[bi] = (s_, str_)
                        else:
                            s_, str_ = st2[bi]
                            s2_, str2_ = sring.next()
                            k.op("act", lambda e: e.activation(out=s2_[:, :n], in_=ps[:, :n], func=AF.Sigmoid), R=[ptr], W=[str2_])
                            k.opw("dve", lambda e: e.tensor_tensor(out=sbr[:, i, t0:t0 + n], in0=s_[:, :n], in1=s2_[:, :n], op=ALU.mult),
                                  R=[str_, str2_], W=[t_sbr], keep=[t_sbr])
                    lin(glu_w, range(32), 8, lambda kc, t0, n: (aT[:, kc, t0:t0 + n], t_aT), TBL, epi_glu)

                    def epi_gs(cc, bi, t0, n, ps, ptr):
                        i = cc - 56
                        m_, mtr_ = mring.next()
                        k.op("sp", lambda e: e.dma_start(out=m_[:, :n], in_=mrs[:, i, t0:t0 + n]), R=[t_mrs], W=[mtr_], dsem=mtr_)
                        s_, str_ = sring.next()
                        k.op("act", lambda e: e.activation(out=s_[:, :n], in_=ps[:, :n], func=AF.Sigmoid), R=[ptr], W=[str_])
                        k.op("dve", lambda e: e.tensor_tensor(out=s_[:, :n], in0=s_[:, :n], in1=sbr[:, i, t0:t0 + n], op=ALU.mult), R=[str_, t_sbr], W=[str_])
                        k.opw("dve", lambda e: e.tensor_tensor(out=mb[:, i, t0:t0 + n], in0=s_[:, :n], in1=m_[:, :n], op=ALU.add),
                              R=[str_, mtr_], W=[t_mb], keep=[t_mb])
                    lin(mix_in, range(56, 72), KC, rhs_u, TBL, epi_gs)
                    k.barrier()
                om = k.sb([128, KC, TL], F32, pz)
                t_om = Trk()

                def epi_mo(cc, bi, t0, n, ps, ptr):
                    k.opw("act", lambda e: e.activation(out=om[:, cc, t0:t0 + n], in_=ps[:, :n], func=AF.Identity), R=[ptr], W=[t_om], keep=[t_om])
                lin(mixo_w, range(16), KC, lambda kc, t0, n: (mb[:, kc, t0:t0 + n], t_mb), TBL, epi_mo)
                if MODE == "mix":
                    k.op("sp", lambda e: e.dma_start(out=dbgm[:, :, :], in_=om[:, :, :]), R=[t_om], W=[t_dbg], dsem=t_dbg)
                    k.barrier()
                    return nc
                t_x2st = Trk()

                def sink2(kc, t0, n):
                    k.opw("sp", lambda e: e.dma_start(out=x2s[:, kc, t0:t0 + n], in_=om[:, kc, t0:t0 + n]), R=[t_om], W=[t_x2s], dsem=t_x2st, keep=[t_x2s])
                ada_post(1, om, t_om, TBL, (x1s, t_x1s), rstd, t_rstd, sq_ring, tmp_ring, xring, sink2)
                k.barrier()
                if dbg is not None and STAGE == 3:
                    k.op("sp", lambda e: e.dma_start(out=dbg[:, :, 0:TL], in_=x2s[:, :, :]), R=[t_x2s], W=[t_dbg], dsem=t_dbg)
                with ExitStack() as pu:
                    u3 = k.sb([128, KC, TL], BF16, pu)
                    t_u3 = Trk()
                    ada_pre(2, om, t_om, TBL, u3, t_u3, rstd, t_rstd, sq_ring, tmp_ring)
                    k.op("sp", lambda e: e.dma_start(out=u3s[:, :, :], in_=u3[:, :, :]), R=[t_u3], W=[t_u3s], dsem=t_u3s)
                    k.barrier()
        if MODE == "full":
            with ExitStack() as pf:
                u3 = k.sb([128, KC, TL], BF16, pf)
                t_u3 = Trk()
                k.op("sp", lambda e: e.dma_start(out=u3[:, :, :], in_=u3s[:, :, :]), R=[t_u3s], W=[t_u3], dsem=t_u3)
                out2 = k.sb([128, KC, TL], F32, pf)
                t_o2 = Trk()
                ffn(1, u3, t_u3, TBL, out2, t_o2, pf)
                t_outs = Trk()

                def sink3(kc, t0, n):
                    k.opw("sp", lambda e: e.dma_start(out=outT[:, kc, t0:t0 + n], in_=out2[:, kc, t0:t0 + n]), R=[t_o2], W=[t_out], dsem=t_outs, keep=[t_out])
                ada_post(2, out2, t_o2, TBL, (x2s, t_x2s), rstd, t_rstd, sq_ring, tmp_ring, xring, sink3)
                k.barrier()

        if dbg is not None and STAGE == 1:
            k.op("sp", lambda e: e.dma_start(out=dbg[:, :, :], in_=x1s[:, :, :]), R=[t_x1s], W=[t_dbg], dsem=t_dbg)
        k.barrier()
    return nc


def _relayout_w(w, kcn):
    K, N = w.shape
    a = w.reshape(kcn, 128, N // 128, 128)
    return np.ascontiguousarray(a.transpose(2, 1, 0, 3)).reshape(N // 128, 128, kcn * 128)


def _fm(v, nchunk):
    a = v.reshape(v.shape[:-1] + (nchunk, 128))
    return np.ascontiguousarray(np.moveaxis(a, -1, 0))


_CACHE = {}


def _consts():
    c = {}
    j = np.arange(128, dtype=np.float32)[:, None]
    i = np.arange(128, dtype=np.float32)[None, :]
    cm = np.zeros((128, 8, 128), np.float32)
    cm[:, 0] = np.eye(128, dtype=np.float32)
    sg = (np.arange(128) // 16)
    cm[:, 1] = (sg[None, :] >= sg[:, None]).astype(np.float32)
    cm[:, 2] = (sg[:, None] >= sg[None, :]).astype(np.float32)
    cm[:, 3] = np.maximum(i - j, 0.0)
    cm[:, 4] = (i >= j).astype(np.float32)
    cm[:, 5] = np.maximum(j - i, 0.0)
    cm[:, 6] = (i < j).astype(np.float32)
    cm[:, 7] = np.broadcast_to(i + 1.0, (128, 128))
    c["cmat"] = cm
    cv = np.zeros((128, 8), np.float32)
    cv[:, 0] = 127.0 - j[:, 0]
    cv[:, 1] = j[:, 0]
    cv[:, 2] = 63.0 - j[:, 0]
    cv[:, 3] = np.float32(np.pi / 2)
    c["cvec"] = cv
    selA = np.zeros((64, 128, 128), np.float32)
    selB = np.zeros((64, 128, 128), np.float32)
    for gl in range(8):
        for s_ in range(8):
            for cc in range(16):
                selA[gl * 8 + s_, gl * 16 + cc, s_ * 16 + cc] = 1.0
                selB[gl * 8 + s_, s_ * 16 + cc, gl * 16 + cc] = 1.0
    c["selA"] = np.ascontiguousarray(selA.transpose(1, 0, 2))
    c["selB"] = np.ascontiguousarray(selB.transpose(1, 0, 2))
    pm = np.zeros((128, 128), np.float32)
    for d in range(128):
        if (d % 64) < 32:
            pm[d + 32, d] = -1.0
        else:
            pm[d - 32, d] = 1.0
    c["pmat"] = pm
    return c


def _mp(a):
    a = np.asarray(a, np.float32)
    a = a.reshape((2, 32, 2, 64) + a.shape[3:])
    a = np.moveaxis(a, (2, 3), (0, 1))
    return np.ascontiguousarray(a.reshape((128,) + a.shape[2:]))


def _prep_small(ssm_lam_re, ssm_lam_im, ssm_log_step, ssm_b_re, ssm_b_im, ssm_c_re, ssm_c_im, ssm_d, ret_decay_logit):
    sh = _consts()
    lre, lim = _mp(ssm_lam_re[0]), _mp(ssm_lam_im[0])
    ls = np.broadcast_to(np.asarray(ssm_log_step[0], np.float32)[:, :, None], (2, 64, 64))
    sh["s5lam"] = np.ascontiguousarray(np.stack([lre, lim, _mp(ls)], 1))
    sh["s5b"] = np.ascontiguousarray(np.stack([_mp(ssm_b_re[0]), _mp(ssm_b_im[0])], 1))
    cre = np.asarray(ssm_c_re[0], np.float32).transpose(0, 1, 3, 2)
    cim = np.asarray(ssm_c_im[0], np.float32).transpose(0, 1, 3, 2)
    sh["s5c"] = np.ascontiguousarray(np.stack([_mp(cre), _mp(cim)], 1))
    dd = np.asarray(ssm_d[0], np.float32).reshape(64, 16)
    sh["s5d"] = np.ascontiguousarray(np.broadcast_to(dd.T[None, :, :], (8, 16, 64)).reshape(128, 64))
    sh["decay"] = np.ascontiguousarray(np.broadcast_to(np.asarray(ret_decay_logit[0], np.float32).reshape(1, 16), (128, 16)))
    return sh


def _selj(j):
    s = np.zeros((128, 4), np.float32)
    s[:, j] = 1.0
    return s


def _rope_tables(j):
    inv = (10000.0 ** (-np.arange(32, dtype=np.float32) / 32)).astype(np.float32)
    pos = np.arange(j * TL, (j + 1) * TL)
    row = (pos // 64).astype(np.float32)
    col = (pos % 64).astype(np.float32)
    rc = np.ones((128, T), np.float32)
    rs = np.zeros((128, T), np.float32)
    for d in range(128):
        pv = row if d < 64 else col
        ang = (pv * inv[d % 32]).astype(np.float32)
        rc[d, :TL] = np.cos(ang)
        rs[d, :TL] = np.sin(ang)
    return rc, rs


def kernel(x, c, ctx, c_ctx, ada_w, ada_b, norm_g, ffn_w_in, ffn_w_out, mix_w_in,
           ssm_lam_re, ssm_lam_im, ssm_log_step, ssm_b_re, ssm_b_im, ssm_c_re, ssm_c_im,
           ssm_d, ssm_glu_w, ret_decay_logit, ret_w_proj, mix_w_out):
    x = np.asarray(x, np.float32)
    ctx = np.asarray(ctx, np.float32)
    if "nc" not in _CACHE:
        _CACHE["nc"] = build()
    nc = _CACHE["nc"]
    shared = _prep_small(ssm_lam_re, ssm_lam_im, ssm_log_step, ssm_b_re, ssm_b_im, ssm_c_re, ssm_c_im, ssm_d, ret_decay_logit)
    shared["ada_w"] = _relayout_w(np.asarray(ada_w[0], np.float32), KC)
    shared["ada_b"] = _fm(np.asarray(ada_b[0], np.float32), NADA * KC)
    shared["norm_g"] = _fm(np.asarray(norm_g[0], np.float32), KC)
    fin = []
    fout = []
    for f in range(2):
        w = np.asarray(ffn_w_in[0, f], np.float32)
        r = _relayout_w(w, KC)
        inter = np.empty_like(r)
        inter[0::2] = r[:HC]
        inter[1::2] = r[HC:]
        fin.append(inter)
        wo = np.asarray(ffn_w_out[0, f], np.float32)
        a = wo.reshape(NH, HG, 128, KC, 128)
        fout.append(np.ascontiguousarray(a.transpose(0, 3, 2, 1, 4)).reshape(NH * KC, 128, HG * 128))
    shared["ffn_in"] = np.stack(fin)
    shared["ffn_out"] = np.stack(fout)
    shared["ones"] = np.ones((128, 128), np.float32)
    shared["mix_in"] = _relayout_w(np.asarray(mix_w_in[0], np.float32), KC)
    gl = _relayout_w(np.asarray(ssm_glu_w[0], np.float32), 8)
    gi = np.empty_like(gl)
    gi[0::2] = gl[:16]
    gi[1::2] = gl[16:]
    shared["glu_w"] = gi
    shared["retp_w"] = _relayout_w(np.asarray(ret_w_proj[0], np.float32), KC)
    shared["mixo_w"] = _relayout_w(np.asarray(mix_w_out[0], np.float32), KC)
    in_maps = []
    for core in range(8):
        b, j = core // 4, core % 4
        xt = np.concatenate([x[b, j * TL:(j + 1) * TL], ctx[b, j * TCX:(j + 1) * TCX]], 0)
        m = dict(shared)
        m["xT"] = np.ascontiguousarray(xt.reshape(T, KC, 128).transpose(2, 1, 0))
        cc = np.stack([np.asarray(c[b], np.float32), np.asarray(c_ctx, np.float32)], -1)
        m["cT"] = np.ascontiguousarray(cc.reshape(KC, 128, 2).transpose(1, 0, 2))
        m["selj"] = _selj(j)
        m["ropec"], m["ropes"] = _rope_tables(j)
        in_maps.append(m)
    res = run_bass_kernel_spmd(nc, in_maps, core_ids=list(range(8)))
    _CACHE["res"] = res
    out = np.empty((2, 4096, D), np.float32)
    for core in range(8):
        b, j = core // 4, core % 4
        o = res.results[core]["outT"]
        out[b, j * TL:(j + 1) * TL] = o.transpose(2, 1, 0).reshape(TL, D)
    return out
```
